# Optimizing a Trainium2 kernel written in Bass

```python
import jax, jax.numpy as jnp
from jax import lax
import numpy as np

D_MODEL = 2048
BATCH = 8
SEQ = 4096
DEPTH = 2
DEC_BATCH = 16
DEC_SEQ = 64
PAST_LEN = 4096

CHUNK = 64
N_HEADS = 8
HEAD_DIM = 128
N_KV_HEADS = 2
GROUP = N_HEADS // N_KV_HEADS
ATTN_W = N_HEADS * HEAD_DIM
KV_W = N_KV_HEADS * HEAD_DIM
N_IDX_HEADS = 16
IDX_DIM = 64
IDX_Q_W = N_IDX_HEADS * IDX_DIM
MAX_TOPK = 256
Q_BLOCK = 128
CONV_W = 512
CONV_K = 3
POOL_W = 512
POOL_WINDOWS = (2, 4, 8, 16)
N_POOL_GROUPS = 4
POOL_GROUP = POOL_W // N_POOL_GROUPS
POOL_HIST = 15
ROPE_THETA = 10000.0
EPS = 1e-6

SPLIT_SIZES = (ATTN_W, KV_W, KV_W, IDX_Q_W, IDX_DIM, N_IDX_HEADS, ATTN_W,
               CONV_W, CONV_W, CONV_W, CONV_W,
               POOL_W, POOL_W,
               D_MODEL, D_MODEL, D_MODEL)
IN_W = 2 * ATTN_W + 2 * KV_W + IDX_Q_W + IDX_DIM + N_IDX_HEADS + 4 * CONV_W + 2 * POOL_W + 3 * D_MODEL

kernel_name = "hybrid_dsa_conv_pool_stream_step"


def split_cols(p):
    offs = np.cumsum(np.array(SPLIT_SIZES))[:-1].tolist()
    return jnp.split(p, offs, axis=-1)


def rms_norm(x, g):
    xf = x.astype(jnp.float32)
    y = xf * lax.rsqrt(jnp.mean(xf * xf, axis=-1, keepdims=True) + EPS)
    return (y * g.astype(jnp.float32)).astype(x.dtype)


def rope(x, pos):
    half = x.shape[-1] // 2
    inv = ROPE_THETA ** (-jnp.arange(half, dtype=jnp.float32) / half)
    ang = pos.astype(jnp.float32)[:, None] * inv[None, :]
    cos = jnp.cos(ang)[None, :, None, :]
    sin = jnp.sin(ang)[None, :, None, :]
    x1 = x[..., :half].astype(jnp.float32)
    x2 = x[..., half:].astype(jnp.float32)
    return jnp.concatenate([x1 * cos - x2 * sin, x2 * cos + x1 * sin], axis=-1).astype(x.dtype)


def sparse_attend(q, iq, iw, q_pos, k_all, v_all, ik_all, topk):
    B, Tq = q.shape[:2]
    L = k_all.shape[1]
    dots = jnp.einsum('bthd,bsd->bths', iq.astype(jnp.float32), ik_all.astype(jnp.float32))
    score = jnp.einsum('bths,bth->bts', jax.nn.relu(dots), iw.astype(jnp.float32))
    q_chunk = q_pos // CHUNK
    admissible = (jnp.arange(L) // CHUNK)[None, :] <= q_chunk[:, None]
    score = jnp.where(admissible[None], score, -jnp.inf)
    _, idx = lax.top_k(score, topk)
    valid = (idx // CHUNK) <= q_chunk[None, :, None]
    k_sel = jax.vmap(lambda kb, ib: kb[ib])(k_all, idx)
    v_sel = jax.vmap(lambda vb, ib: vb[ib])(v_all, idx)
    qg = q.reshape(B, Tq, N_KV_HEADS, GROUP, HEAD_DIM).astype(jnp.float32)
    logits = jnp.einsum('btkgd,btnkd->btkgn', qg, k_sel.astype(jnp.float32)) * (HEAD_DIM ** -0.5)
    logits = jnp.where(valid[:, :, None, None, :], logits, -jnp.inf)
    p = jax.nn.softmax(logits, axis=-1)
    o = jnp.einsum('btkgn,btnkd->btkgd', p, v_sel.astype(jnp.float32))
    return o.reshape(B, Tq, ATTN_W).astype(q.dtype)


def prompt_attention(q, iq, iw, pos, k, v, ik, topk):
    B, T = q.shape[:2]
    nb = T // Q_BLOCK

    def blocks(a):
        return jnp.moveaxis(a.reshape((B, nb, Q_BLOCK) + a.shape[2:]), 1, 0)

    def one(args):
        qb, iqb, iwb, pb = args
        return sparse_attend(qb, iqb, iwb, pb, k, v, ik, topk)

    o = lax.map(one, (blocks(q), blocks(iq), blocks(iw), pos.reshape(nb, Q_BLOCK)))
    return jnp.moveaxis(o, 0, 1).reshape(B, T, ATTN_W)


def mixer_layer(x, pos, past, norm_g, w_in, conv_w, conv_b, pool_w, pool_scale,
                lift_a, lift_b, lift_c, w_out, topk):
    B, T, _ = x.shape
    h = rms_norm(x, norm_g)
    (q, k, v, iq, ik, iw, gate_a, u, b_gate, c_gate, gate_b, p_in, gate_c,
     m_a, m_b, m_c) = split_cols(h @ w_in)
    q = rope(q.reshape(B, T, N_HEADS, HEAD_DIM), pos)
    k = rope(k.reshape(B, T, N_KV_HEADS, HEAD_DIM), pos)
    v = v.reshape(B, T, N_KV_HEADS, HEAD_DIM)
    iq = rope(iq.reshape(B, T, N_IDX_HEADS, IDX_DIM), pos)
    ik = rope(ik.reshape(B, T, 1, IDX_DIM), pos)[:, :, 0]
    iw = iw * ((IDX_DIM ** -0.5) * (N_IDX_HEADS ** -0.5))
    if past is None:
        conv_hist = jnp.zeros((B, CONV_K - 1, CONV_W), x.dtype)
        pool_hist = jnp.zeros((B, POOL_HIST, POOL_W), x.dtype)
        attn = prompt_attention(q, iq, iw, pos, k, v, ik, topk)
    else:
        ck, cv, cik, conv_hist, pool_hist = past
        k_all = jnp.concatenate([ck.astype(k.dtype), k], axis=1)
        v_all = jnp.concatenate([cv.astype(v.dtype), v], axis=1)
        ik_all = jnp.concatenate([cik.astype(ik.dtype), ik], axis=1)
        attn = sparse_attend(q, iq, iw, pos, k_all, v_all, ik_all, topk)
    y_a = attn * jax.nn.silu(gate_a)
    cin = c_gate * u
    xc = jnp.concatenate([conv_hist.astype(cin.dtype), cin], axis=1)
    conv = (xc[:, 0:T] * conv_w[0] + xc[:, 1:T + 1] * conv_w[1]
            + xc[:, 2:T + 2] * conv_w[2] + conv_b)
    y_b = b_gate * conv * jax.nn.silu(gate_b)
    xp = jnp.concatenate([pool_hist.astype(p_in.dtype), p_in], axis=1)
    cs = jnp.concatenate([jnp.zeros((B, 1, POOL_W), jnp.float32),
                          jnp.cumsum(xp.astype(jnp.float32), axis=1)], axis=1)
    end = cs[:, POOL_HIST + 1:]
    means = []
    for g, w in enumerate(POOL_WINDOWS):
        sl = slice(g * POOL_GROUP, (g + 1) * POOL_GROUP)
        start = cs[:, POOL_HIST + 1 - w:POOL_HIST + 1 - w + T, sl]
        cnt = jnp.minimum(w, pos + 1).astype(jnp.float32)[None, :, None]
        means.append((end[..., sl] - start) / cnt)
    pooled = jnp.concatenate(means, axis=-1)
    d = (pooled - p_in.astype(jnp.float32)).reshape(B, T, N_POOL_GROUPS, POOL_GROUP)
    mixed = jnp.einsum('btgc,gcd->btgd', d, pool_w.astype(jnp.float32)).reshape(B, T, POOL_W)
    y_c = (mixed * pool_scale.astype(jnp.float32)).astype(x.dtype) * jax.nn.silu(gate_c)
    z = (jax.nn.sigmoid(m_a) * (y_a @ lift_a) + jax.nn.sigmoid(m_b) * (y_b @ lift_b)
         + jax.nn.sigmoid(m_c) * (y_c @ lift_c))
    out = x + z @ w_out
    new_state = (k, v, ik, xc[:, -(CONV_K - 1):], xp[:, -POOL_HIST:])
    return out, new_state


def setup_inputs(seed: int = 0) -> dict:
    key = jax.random.key(seed)
    ks = jax.random.split(key, 20)
    f = jnp.float32
    nrm = lambda k, s, sc: jax.random.normal(k, s, f) * sc
    return {
        "x_prompt": nrm(ks[0], (BATCH, SEQ, D_MODEL), 1.0),
        "x_sample": nrm(ks[1], (DEC_BATCH, DEC_SEQ, D_MODEL), 1.0),
        "cache_k": nrm(ks[2], (DEPTH, DEC_BATCH, PAST_LEN, N_KV_HEADS, HEAD_DIM), 1.0),
        "cache_v": nrm(ks[3], (DEPTH, DEC_BATCH, PAST_LEN, N_KV_HEADS, HEAD_DIM), 1.0),
        "cache_idx_k": nrm(ks[4], (DEPTH, DEC_BATCH, PAST_LEN, IDX_DIM), 1.0),
        "state_conv": nrm(ks[5], (DEPTH, DEC_BATCH, CONV_K - 1, CONV_W), 1.0),
        "state_pool": nrm(ks[6], (DEPTH, DEC_BATCH, POOL_HIST, POOL_W), 1.0),
        "norm_g": 1.0 + nrm(ks[7], (DEPTH, D_MODEL), 0.01),
        "w_in": nrm(ks[8], (DEPTH, D_MODEL, IN_W), D_MODEL ** -0.5),
        "conv_w": nrm(ks[9], (DEPTH, CONV_K, CONV_W), CONV_K ** -0.5),
        "conv_b": nrm(ks[10], (DEPTH, CONV_W), 0.01),
        "pool_w": nrm(ks[11], (DEPTH, N_POOL_GROUPS, POOL_GROUP, POOL_GROUP), POOL_GROUP ** -0.5),
        "pool_scale": 1.0 + nrm(ks[12], (DEPTH, POOL_W), 0.1),
        "lift_a": nrm(ks[13], (DEPTH, ATTN_W, D_MODEL), ATTN_W ** -0.5),
        "lift_b": nrm(ks[14], (DEPTH, CONV_W, D_MODEL), CONV_W ** -0.5),
        "lift_c": nrm(ks[15], (DEPTH, POOL_W, D_MODEL), POOL_W ** -0.5),
        "w_out": nrm(ks[16], (DEPTH, D_MODEL, D_MODEL), D_MODEL ** -0.5),
        "final_norm_g": 1.0 + nrm(ks[17], (D_MODEL,), 0.01),
    }


def reference(x_prompt, x_sample, cache_k, cache_v, cache_idx_k, state_conv, state_pool,
              norm_g, w_in, conv_w, conv_b, pool_w, pool_scale, lift_a, lift_b, lift_c,
              w_out, final_norm_g):
    seq = x_prompt.shape[1]
    past_len = cache_k.shape[2]
    dec_seq = x_sample.shape[1]
    topk_prompt = min(MAX_TOPK, seq // 4)
    topk_sample = min(MAX_TOPK, (past_len + dec_seq) // 4)
    pos_prompt = jnp.arange(seq, dtype=jnp.int32)
    pos_sample = past_len + jnp.arange(dec_seq, dtype=jnp.int32)

    hp, hs = x_prompt, x_sample
    p_states = ([], [], [], [], [])
    s_states = ([], [], [], [], [])
    for l in range(DEPTH):
        params = (norm_g[l], w_in[l], conv_w[l], conv_b[l], pool_w[l], pool_scale[l],
                  lift_a[l], lift_b[l], lift_c[l], w_out[l])
        hp, st_p = mixer_layer(hp, pos_prompt, None, *params, topk_prompt)
        past = (cache_k[l], cache_v[l], cache_idx_k[l], state_conv[l], state_pool[l])
        hs, st_s = mixer_layer(hs, pos_sample, past, *params, topk_sample)
        for i in range(5):
            p_states[i].append(st_p[i])
            s_states[i].append(st_s[i])
    y_prompt = rms_norm(hp, final_norm_g)
    y_sample = rms_norm(hs, final_norm_g)
    k_prompt = jnp.stack(p_states[0])
    v_prompt = jnp.stack(p_states[1])
    idxk_prompt = jnp.stack(p_states[2])
    conv_prompt = jnp.stack(p_states[3])
    pool_prompt = jnp.stack(p_states[4])
    k_sample = jnp.stack(s_states[0])
    v_sample = jnp.stack(s_states[1])
    idxk_sample = jnp.stack(s_states[2])
    conv_sample = jnp.stack(s_states[3])
    pool_sample = jnp.stack(s_states[4])
    return (y_prompt, y_sample, k_prompt, v_prompt, idxk_prompt, conv_prompt, pool_prompt,
            k_sample, v_sample, idxk_sample, conv_sample, pool_sample)
```

```python
import contextlib
import numpy as np
import concourse.bass as bass
import concourse.mybir as mybir
from concourse.bass_utils import run_bass_kernel_spmd

F32 = mybir.dt.float32
BF16 = mybir.dt.bfloat16
ALU = mybir.AluOpType
AF = mybir.ActivationFunctionType
AX = mybir.AxisListType

N_CORES = 8
CHUNK = 64
HEAD_DIM = 128
ATTN_W = 1024
KV_W = 256
IDX_DIM = 64
N_IDX = 16
CONV_W = 512
POOL_W = 512
POOL_HIST = 15
MAX_TOPK = 256
EPS = 1e-6
THETA = 10000.0
NIT = 16
NEG = -1.0e30


class Cfg:
    def __init__(self, D=2048, SEQ=4096, PAST=4096, DEC=64, DEPTH=2, debug=False):
        self.debug = debug
        self.D, self.SEQ, self.PAST, self.DEC, self.DEPTH = D, SEQ, PAST, DEC, DEPTH
        self.KC = D // 128
        self.TM_W = ATTN_W + 2 * KV_W + N_IDX * IDX_DIM + IDX_DIM + N_IDX
        self.FM_W = ATTN_W + 4 * CONV_W + 2 * POOL_W + 3 * D
        self.IN_W = self.TM_W + self.FM_W
        assert self.FM_W % 512 == 0 and D % 512 == 0
        self.NFM = self.FM_W // 512
        self.ND = D // 512
        self.NTM = (self.TM_W + 511) // 512
        self.NBLK = self.NTM + self.NFM + 2 * self.ND
        self.TOPK_P = min(MAX_TOPK, SEQ // 4)
        self.TOPK_S = min(MAX_TOPK, (PAST + DEC) // 4)
        self.KMAX = max(SEQ, PAST + DEC)
        self.NKT = (self.KMAX + 127) // 128


class Res:
    __slots__ = ("w", "r", "name", "excl")

    def __init__(self, name="", excl=False):
        self.w = None
        self.r = {}
        self.name = name
        self.excl = excl


class Eng:
    def __init__(self, name, semkey, is_pe=False):
        self.name, self.semkey, self.is_pe = name, semkey, is_pe
        self.cnt = 0
        self.seen = {}
        self.prog = []
        self.ring = []
        self.ndma = 0


class _Rec:
    def __init__(self):
        self.call = None

    def __getattr__(self, name):
        def f(*a, **k):
            self.call = (name, a, k)
            return self
        return f


def _capture(fn):
    r = _Rec()
    fn(r)
    assert r.call is not None
    return r.call


class Sched:
    def __init__(self):
        self.sems = []
        self.engs = {}

    def new_sem(self, name):
        self.sems.append(name)
        return len(self.sems) - 1

    def engine(self, name, is_pe=False, ring=0):
        e = Eng(name, self.new_sem("s_" + name), is_pe)
        e.ring = [self.new_sem("r_%s%d" % (name, i)) for i in range(ring)]
        self.engs[name] = e
        return e

    def _deps(self, e, R, W):
        deps = {}

        def need(tok, raw):
            if tok is None:
                return
            sk, v = tok
            if sk == e.semkey and e.is_pe:
                return
            if deps.get(sk, 0) < v:
                deps[sk] = v
        for r in R:
            need(r.w, True)
        for w in W:
            need(w.w, False)
            for sk, v in w.r.items():
                need((sk, v), False)
        for sk, v in deps.items():
            if e.seen.get(sk, 0) < v:
                e.prog.append(("wait", sk, v))
                e.seen[sk] = v

    def _mark(self, tok, R, W):
        for r in R:
            if r.r.get(tok[0], 0) < tok[1]:
                r.r[tok[0]] = tok[1]
        for w in W:
            w.w = tok
            w.r = {}

    def op(self, e, fn, R=(), W=(), inc=True, drain=False):
        pd = getattr(e, "pending_drain", 0)
        if pd:
            if e.seen.get(e.semkey, 0) < pd:
                e.prog.append(("wait", e.semkey, pd))
                e.seen[e.semkey] = pd
            e.pending_drain = 0
        if drain:
            e.pending_drain = e.cnt + 1
        if any(r.excl for r in R):
            W = list(W) + [r for r in R if r.excl]
            R = [r for r in R if not r.excl]
        self._deps(e, R, W)
        tok = (e.semkey, e.cnt + 1)
        e.prog.append(("op", _capture(fn), e.semkey if inc else None, 1))
        if inc:
            e.cnt += 1
        self._mark(tok, R, W)

    def dma(self, e, fn, R=(), W=()):
        n = e.ndma
        e.ndma += 1
        G = len(e.ring)
        sk = e.ring[n % G]
        prev = 16 * (n // G)
        if prev > 0 and e.seen.get(sk, 0) < prev:
            e.prog.append(("wait", sk, prev))
            e.seen[sk] = prev
        self._deps(e, R, W)
        tok = (sk, prev + 16)
        e.prog.append(("op", _capture(fn), sk, 16))
        self._mark(tok, R, W)

    def finish(self, e):
        for i, sk in enumerate(e.ring):
            n_on = (e.ndma - i + len(e.ring) - 1) // len(e.ring) if e.ndma > i else 0
            if n_on > 0:
                e.prog.append(("wait", sk, 16 * n_on))


def build(cfg):
    c = cfg
    D, KC, SEQ, PAST, DEC, DEPTH = c.D, c.KC, c.SEQ, c.PAST, c.DEC, c.DEPTH
    KMAX, NKT = c.KMAX, c.NKT
    nc = bass.Bass("TRN2", target_bir_lowering=False)

    def din(name, shape, dt=F32):
        return nc.dram_tensor(name, list(shape), dt, kind="ExternalInput").ap()

    def dout(name, shape):
        return nc.dram_tensor(name, list(shape), F32, kind="ExternalOutput").ap()

    def dint(name, shape, dt):
        return nc.dram_tensor(name, list(shape), dt, kind="Internal").ap()

    xp = din("xp", [SEQ, D])
    xs_in = din("xs", [2 * DEC, D])
    ck = din("ck", [DEPTH, 2, PAST, KV_W])
    cv = din("cv", [DEPTH, 2, PAST, KV_W])
    cik = din("cik", [DEPTH, 2, PAST, IDX_DIM])
    sconv = din("sconv", [DEPTH, 2, 2, CONV_W])
    spool = din("spool", [DEPTH, 2, POOL_HIST, POOL_W])
    norm_g = din("norm_g", [DEPTH, D])
    w_in = din("w_in", [DEPTH, D, c.IN_W])
    conv_w = din("conv_w", [DEPTH, 3, CONV_W])
    conv_b = din("conv_b", [DEPTH, CONV_W])
    pool_w = din("pool_w", [DEPTH, 4, 128, 128])
    pool_scale = din("pool_scale", [DEPTH, POOL_W])
    lift_a = din("lift_a", [DEPTH, ATTN_W, D])
    lift_b = din("lift_b", [DEPTH, CONV_W, D])
    lift_c = din("lift_c", [DEPTH, POOL_W, D])
    w_out = din("w_out", [DEPTH, D, D])
    fng = din("fng", [D])
    ident_in = din("ident", [128, 128])
    ropeq_p = din("ropeq_p", [SEQ, 128])
    ropei_p = din("ropei_p", [SEQ, 64])
    ropeq_s = din("ropeq_s", [2 * DEC, 128])
    ropei_s = din("ropei_s", [2 * DEC, 64])
    invc_in = din("invc", [128, 4, 16])
    pow2_in = din("pow2", [128, NIT])

    yp = dout("yp", [SEQ, D])
    ys = dout("ys", [2 * DEC, D])
    kp = dout("kp", [DEPTH, SEQ, KV_W])
    vp = dout("vp", [DEPTH, SEQ, KV_W])
    ikp = dout("ikp", [DEPTH, SEQ, IDX_DIM])
    convp = dout("convp", [DEPTH, 2, CONV_W])
    poolp = dout("poolp", [DEPTH, POOL_HIST, POOL_W])
    ks = dout("ks", [DEPTH, 2 * DEC, KV_W])
    vs = dout("vs", [DEPTH, 2 * DEC, KV_W])
    iks = dout("iks", [DEPTH, 2 * DEC, IDX_DIM])
    convs = dout("convs", [DEPTH, 2, 2, CONV_W])
    pools = dout("pools", [DEPTH, 2, POOL_HIST, POOL_W])

    wsc = dint("wsc", [DEPTH, c.NBLK, 128, KC * 512], BF16)
    if c.debug:
        x1p = dout("x1p", [SEQ, D])
        x1s = dout("x1s", [2 * DEC, D])
    else:
        x1p = dint("x1p", [SEQ, D], F32)
        x1s = dint("x1s", [2 * DEC, D], F32)
    dbg_names = []

    def dump(name, ap, R):
        if not c.debug:
            return
        d = dout(name, list(ap.shape))
        dbg_names.append(name)
        GDMA(d, ap, R=R)

    S = Sched()
    marks = []
    build.marks = marks

    def mark(label):
        marks.append((label, sum(1 for x in pe.prog if x[0] == 'op')))
    pe = S.engine("pe", is_pe=True)
    act = S.engine("act")
    dve = S.engine("dve")
    pool = S.engine("pool", ring=8)
    sp = S.engine("sp", ring=8)

    es = contextlib.ExitStack()

    def sb(name, shape, dt):
        return es.enter_context(nc.sbuf_tensor("t_" + name, list(shape), dt))

    TTMAX = 512
    ident_f = sb("ident_f", [128, 128], F32)
    ident = sb("ident", [128, 128], BF16)
    ones = sb("ones", [128, 128], BF16)
    gT = sb("gT", [128, DEPTH, KC], F32)
    cwT = sb("cwT", [128, DEPTH, 3, 4], F32)
    cbT = sb("cbT", [128, DEPTH, 4], F32)
    psT = sb("psT", [128, DEPTH, 4], F32)
    poolw = sb("poolw", [128, DEPTH, 4, 128], BF16)
    invc = sb("invc_sb", [128, 4, 16], F32)
    pow2 = sb("pow2_sb", [128, NIT], F32)
    KT = sb("KT", [128, 2, KMAX], BF16)
    V = sb("V", [128, NKT, KV_W], BF16)
    IKT = sb("IKT", [128, KMAX], BF16)
    hT = sb("hT", [128, KC, TTMAX], BF16)
    attnT = sb("attnT", [128, 8, TTMAX], BF16)
    wslot = [sb("wslot%d" % i, [128, KC, 512], BF16) for i in range(4)]
    ropeq = sb("ropeq", [128, 4, 128], F32)
    ropei = sb("ropei", [128, 4, 64], F32)
    ikst = [sb("ikst%d" % i, [128, IDX_DIM], F32) for i in range(2)]
    ikdup2 = [sb("ikdup%d" % i, [128, 2, IDX_DIM], BF16) for i in range(2)]
    absw = sb("absw", [128, 4, N_IDX], F32)
    sgn = sb("sgn", [128, 4, N_IDX], F32)
    t16buf = sb("t16buf", [128, 16], F32)
    small = sb("small", [128, 16], F32)
    bar_t = sb("bar_t", [128, 2], F32)
    whs = sb("whs", [128, NIT], F32)
    bs = sb("bs", [128, 8], F32)
    rden = sb("rden", [128, 512], F32)
    hist_c = sb("hist_c", [128, 4, 2], F32)
    hist_p = sb("hist_p", [128, 4, POOL_HIST], F32)
    knew = sb("knew", [128, 2, 2, 64], BF16)
    vnew = sb("vnew", [64, 2, KV_W], BF16)
    iknew = sb("iknew", [128, 2, 64], BF16)
    mneg = sb("mneg", [128, KMAX], BF16)
    I4p = sb("I4p", [128, 4, 128], BF16)
    I4s = sb("I4s", [64, 4, 64], BF16)
    UB = 57344
    U = sb("U", [128, UB // 2], BF16)

    class Carve:
        def __init__(self):
            self.off = 0

        def take(self, shape, dt):
            n = int(np.prod(shape[1:]))
            esz = 4 if dt == F32 else 2
            assert self.off % 4 == 0
            a = U[:, self.off // 2: self.off // 2 + n * esz // 2]
            self.off += n * esz
            assert self.off <= UB, (self.off, UB)
            if dt == F32:
                a = a.bitcast(F32)
            if len(shape) > 2:
                names = " ".join("d%d" % i for i in range(len(shape) - 1))
                kw = {"d%d" % i: shape[i + 1] for i in range(len(shape) - 1)}
                a = a.rearrange("p (%s) -> p %s" % (names, names), **kw)
            return a

    cb_ = Carve()
    QT = cb_.take([128, 8, TTMAX], BF16)
    IQT = cb_.take([128, 8, TTMAX], BF16)
    diag = cb_.take([128, N_IDX, 128], BF16)
    ab_off = cb_.off
    scores = cb_.take([128, KMAX], F32)
    Rbuf_off = cb_.off
    Rbuf = [cb_.take([128, 512], BF16) for _ in range(6)]
    EP_off = cb_.off
    EP = [cb_.take([128, 512], BF16) for _ in range(6)]
    kstg = cb_.take([128, 8, KV_W], BF16)
    ikstg = cb_.take([128, 8, 2, IDX_DIM], BF16)
    ca_ = Carve()
    ca_.off = ab_off
    xsb = [ca_.take([128, D], F32) for _ in range(2)]
    xnb = [ca_.take([128, D], BF16) for _ in range(2)]
    tmf = [ca_.take([128, 512], F32) for _ in range(2)]
    tmb = [ca_.take([128, 512], BF16) for _ in range(2)]
    rtmp = [ca_.take([128, 256], F32) for _ in range(2)]
    kst = [ca_.take([128, KV_W], F32) for _ in range(2)]
    vst = [ca_.take([128, KV_W], F32) for _ in range(2)]
    cc_ = Carve()
    zT = cc_.take([128, KC, TTMAX], BF16)
    y_a = cc_.take([128, 8, TTMAX], BF16)
    y_b = cc_.take([128, 4, TTMAX], BF16)
    y_c = cc_.take([128, 4, TTMAX], BF16)
    c4_off = cc_.off
    tmpf = [cc_.take([128, POOL_HIST + TTMAX + 1], F32) for _ in range(2)]
    dbf = [cc_.take([128, TTMAX], BF16) for _ in range(2)]
    c4b_off = cc_.off
    ubuf = cc_.take([128, 4, TTMAX], F32)
    cin = cc_.take([128, 4, 2 + TTMAX], F32)
    cc_.off = c4b_off
    pbuf = cc_.take([128, 4, POOL_HIST + TTMAX + 1], F32)
    mixs = cc_.take([128, 4, TTMAX], F32)
    wa, wb = tmpf[0], tmpf[1]
    cc_.off = c4_off
    zacc = cc_.take([128, 4, TTMAX], F32)
    sigb = [cc_.take([128, TTMAX], F32) for _ in range(2)]
    ztmp = [cc_.take([128, TTMAX], F32) for _ in range(2)]
    cc_.off = 16384
    xo = cc_.take([128, 4, D], F32)
    gfin = cc_.take([128, D], F32)

    banks = [es.enter_context(nc.psum_tensor("bank%d" % i, [128, 512], F32)) for i in range(8)]
    bres = [Res("bank%d" % i, excl=True) for i in range(8)]

    TU = Res("TU")
    r_const = Res("const")
    r_KT, r_V, r_IKT = Res("KT"), Res("V"), Res("IKT")
    r_hT = Res("hT")
    r_attn = Res("attnT")
    r_w = [Res("w%d" % i) for i in range(4)]
    r_rope = Res("rope")
    r_tmf = [Res(), Res()]
    r_tmb = [Res(), Res()]
    r_rtmp = Res()
    r_kst = [Res(), Res()]
    r_vst = [Res(), Res()]
    r_ikst = [Res(), Res()]
    r_ikdup2 = [Res(), Res()]
    r_iw = Res()
    r_small = Res()
    r_whs = Res()
    r_bs = [Res() for _ in range(8)]
    r_rden = Res()
    r_histc, r_histp = Res(), Res()
    r_new = Res()
    r_kstg, r_ikstg = Res(), Res()
    r_mneg = Res()
    r_scores = Res()
    r_R = [Res() for _ in range(6)]
    r_EP = [Res() for _ in range(6)]
    r_QT, r_IQT, r_diag = Res(), Res(), Res()
    r_xsb, r_xnb = [Res(), Res()], [Res(), Res()]
    r_zT, r_ya, r_yb, r_yc = Res(), Res(), Res(), Res()
    r_ubuf, r_cin, r_pbuf, r_mixs = Res(), Res(), Res(), Res()
    r_tmpf = [Res(), Res()]
    r_wa, r_wb = r_tmpf[0], r_tmpf[1]
    r_gfin = Res()
    r_dbf = [Res(), Res()]
    r_zacc = Res()
    r_sig = [Res(), Res()]
    r_ztmp = [Res(), Res()]
    r_xo = [Res() for _ in range(4)]
    r_wsc = [[Res() for _ in range(c.NBLK)] for _ in range(DEPTH)]
    r_x1 = Res()

    cnt = {"tmf": 0, "tmb": 0, "st": 0, "R": 0, "EP": 0, "tmpf": 0, "dbf": 0, "sig": 0, "zt": 0,
           "mm": 0, "tr": 0, "wslot": 0, "dots": 0, "sbk": 0, "c6": 0}

    def nxt(key, n):
        v = cnt[key] % n
        cnt[key] += 1
        return v

    cnt["mm4"] = 0
    cnt["tr4"] = 0

    def mmbank():
        return [0, 1, 4, 5][nxt("mm4", 4)]

    def trbank():
        return [2, 3, 6, 7][nxt("tr4", 4)]

    def barrier():
        S.op(dve, lambda h: h.memset(bar_t[0:1, 0:1], 0.0), W=[TU])

    def DVE(fn, R=(), W=()):
        S.op(dve, fn, R=R, W=W)

    def ACT(fn, R=(), W=(), drain=False):
        S.op(act, fn, R=R, W=W, drain=drain)

    def POOL(fn, R=(), W=()):
        S.op(pool, fn, R=R, W=W)

    def PE(fn, R=(), W=(), inc=True):
        S.op(pe, fn, R=R, W=W, inc=inc)

    def SP(out, in_, R=(), W=(), slow=False):
        if slow:
            S.dma(sp, lambda h: h.dma_start(out=out, in_=in_, allow_slow_non_contiguous=True), R=R, W=W)
        else:
            S.dma(sp, lambda h: h.dma_start(out=out, in_=in_), R=R, W=W)

    def GDMA(out, in_, R=(), W=()):
        S.dma(pool, lambda h: h.dma_start(out=out, in_=in_), R=R, W=W)

    def bank_bf(b):
        return banks[b][:, :].bitcast(BF16)

    def blk_tm(i): return i
    def blk_fm(i): return c.NTM + i
    def blk_lift(i): return c.NTM + c.NFM + i
    def blk_out(i): return c.NTM + c.NFM + c.ND + i

    def wsc_view(l, b):
        return wsc[l, b].rearrange("p (k n) -> p k n", k=KC)

    r_liftp = [[[Res() for _ in range(3)] for _ in range(c.ND)] for _ in range(DEPTH)]

    def convert_layer2(l):
        def wsrc(mat, nk, n0, w):
            return mat[0:nk * 128, n0:n0 + w].rearrange("(k p) n -> p k n", p=128)
        for i in range(c.NTM):
            n0 = i * 512
            w = min(512, c.TM_W - n0)
            GDMA(wsc_view(l, blk_tm(i))[:, :, 0:w], wsrc(w_in[l], KC, n0, w), W=[r_wsc[l][blk_tm(i)]])
        for i in range(c.NFM):
            n0 = c.TM_W + i * 512
            GDMA(wsc_view(l, blk_fm(i)), wsrc(w_in[l], KC, n0, 512), W=[r_wsc[l][blk_fm(i)]])
        for i in range(c.ND):
            n0 = i * 512
            v = wsc_view(l, blk_lift(i))
            GDMA(v[:, 0:8, :], wsrc(lift_a[l], 8, n0, 512), W=[r_liftp[l][i][0]])
            GDMA(v[:, 8:12, :], wsrc(lift_b[l], 4, n0, 512), W=[r_liftp[l][i][1]])
            GDMA(v[:, 12:16, :], wsrc(lift_c[l], 4, n0, 512), W=[r_liftp[l][i][2]])
        for i in range(c.ND):
            n0 = i * 512
            GDMA(wsc_view(l, blk_out(i)), wsrc(w_out[l], KC, n0, 512), W=[r_wsc[l][blk_out(i)]])

    def load_w(l, b, w=512, slot=None):
        s = nxt("wslot", 4) if slot is None else slot
        Rr = [r_wsc[l][b]]
        if c.NTM + c.NFM <= b < c.NTM + c.NFM + c.ND:
            Rr = r_liftp[l][b - c.NTM - c.NFM]
        SP(wslot[s][:, :, 0:w], wsc_view(l, b)[:, :, 0:w], R=Rr, W=[r_w[s]])
        return wslot[s], r_w[s]

    SP(ident_f[:, :], ident_in, W=[r_const])
    DVE(lambda h: h.tensor_copy(out=ident[:, :], in_=ident_f[:, :]), R=[r_const], W=[r_const])
    DVE(lambda h: h.memset(ones[:, :], 1.0), W=[r_const])
    DVE(lambda h: h.tensor_copy(out=I4p[:, :, :], in_=ident[:, None, :].to_broadcast([128, 4, 128])), R=[r_const], W=[r_const])
    DVE(lambda h: h.tensor_copy(out=I4s[:, :, :], in_=ident[0:64, None, 0:64].to_broadcast([64, 4, 64])), R=[r_const], W=[r_const])
    DVE(lambda h: h.memset(small[:, :], 0.0), W=[r_small])
    SP(invc[:, :, :], invc_in, W=[r_const])
    SP(pow2[:, :], pow2_in, W=[r_const])
    for l in range(DEPTH):
        SP(gT[:, l, :], norm_g[l].rearrange("(k p) -> p k", p=128), W=[r_const], slow=True)
        SP(cbT[:, l, :], conv_b[l].rearrange("(k p) -> p k", p=128), W=[r_const], slow=True)
        SP(psT[:, l, :], pool_scale[l].rearrange("(k p) -> p k", p=128), W=[r_const], slow=True)
        for j in range(3):
            SP(cwT[:, l, j, :], conv_w[l, j].rearrange("(k p) -> p k", p=128), W=[r_const], slow=True)
        GDMA(poolw[:, l, :, :], pool_w[l].rearrange("g c d -> c g d"), W=[r_const])
    for l in range(DEPTH):
        convert_layer2(l)

    class Tile:
        pass

    def prompt_tile(t):
        T = Tile()
        T.kind = "p"
        T.idx = t
        T.SUB = 128
        T.NT = min(4, SEQ // 128 - 4 * t)
        T.TT = T.NT * 128
        T.tok0 = t * 512
        T.NSEG = 1
        T.SEGL = T.TT
        T.first = (t == 0)
        T.last = (T.tok0 + T.TT == SEQ)
        return T

    def sample_tile():
        T = Tile()
        T.kind = "s"
        T.idx = 0
        T.SUB = DEC
        T.NT = 2
        T.TT = 2 * DEC
        T.tok0 = 0
        T.NSEG = 2
        T.SEGL = DEC
        T.first = False
        T.last = True
        return T

    tiles = [prompt_tile(t) for t in range((SEQ + 511) // 512)] + [sample_tile()]

    def rope(src4, cos, sin, dst4, H, half, SUBn, Rsrc, Wdst):
        x1, x2 = src4[:, :, 0, :], src4[:, :, 1, :]
        cb = cos[:, None, :].to_broadcast([SUBn, H, half])
        sb_ = sin[:, None, :].to_broadcast([SUBn, H, half])
        n = H * half
        t1 = rtmp[0][0:SUBn, 0:n].rearrange("p (h d) -> p h d", h=H)
        t2 = rtmp[1][0:SUBn, 0:n].rearrange("p (h d) -> p h d", h=H)
        DVE(lambda h: h.tensor_tensor(out=t1, in0=x1, in1=cb, op=ALU.mult), R=Rsrc + [r_rope, TU], W=[r_rtmp])
        DVE(lambda h: h.tensor_tensor(out=t2, in0=x2, in1=sb_, op=ALU.mult), R=Rsrc + [r_rope, TU], W=[r_rtmp])
        DVE(lambda h: h.tensor_tensor(out=dst4[:, :, 0, :], in0=t1, in1=t2, op=ALU.subtract), R=[r_rtmp, TU], W=Wdst)
        DVE(lambda h: h.tensor_tensor(out=t1, in0=x2, in1=cb, op=ALU.mult), R=Rsrc + [r_rope, TU], W=[r_rtmp])
        DVE(lambda h: h.tensor_tensor(out=t2, in0=x1, in1=sb_, op=ALU.mult), R=Rsrc + [r_rope, TU], W=[r_rtmp])
        DVE(lambda h: h.tensor_tensor(out=dst4[:, :, 1, :], in0=t1, in1=t2, op=ALU.add), R=[r_rtmp, TU], W=Wdst)

    def transposes(src_fn, n, rows, SUBn, evac_fn, Rsrc):
        b = trbank()
        pv = bank_bf(b)[:, 0:n * SUBn].rearrange("p (j t) -> p j t", j=n)
        for j in range(n):
            srcj = src_fn(j)
            PE(lambda h, srcj=srcj, j=j: h.transpose(out=pv[0:rows, j, :], in_=srcj, identity=ident[0:SUBn, 0:SUBn]),
               R=Rsrc + [r_const], W=[bres[b]], inc=(j == n - 1))
        evac_fn(pv, bres[b])

    def process(l, T):
        SUBn, NT, TT = T.SUB, T.NT, T.TT
        last_layer = (l == DEPTH - 1)
        if T.kind == "p":
            xsrc = xp if l == 0 else x1p
            xdst = yp if last_layer else x1p
            rq_src, ri_src = ropeq_p, ropei_p
            k_dst, v_dst, ik_dst = kp[l], vp[l], ikp[l]
            topk = c.TOPK_P
        else:
            xsrc = xs_in if l == 0 else x1s
            xdst = ys if last_layer else x1s
            rq_src, ri_src = ropeq_s, ropei_s
            k_dst, v_dst, ik_dst = ks[l], vs[l], iks[l]
            topk = c.TOPK_S

        def rows(st):
            return slice(T.tok0 + st * SUBn, T.tok0 + (st + 1) * SUBn)

        def tl(st):
            return slice(st * SUBn, (st + 1) * SUBn)

        mark('L%d %s%d A' % (l, T.kind, T.idx))
        barrier()
        if T.kind == "p" and T.first:
            DVE(lambda h: h.memset(hist_c[:, :, :], 0.0), W=[r_histc])
            DVE(lambda h: h.memset(hist_p[:, :, :], 0.0), W=[r_histp])
        for st in range(NT):
            xs, r_xs = xsb[st % 2], r_xsb[st % 2]
            xn, r_xn = xnb[st % 2], r_xnb[st % 2]
            ss, rs_ = small[0:SUBn, 12 + (st % 2):13 + (st % 2)], small[0:SUBn, 0 + (st % 2):1 + (st % 2)]
            SP(xs[0:SUBn, :], xsrc[rows(st), :], R=[TU, r_x1], W=[r_xs])
            DVE(lambda h: h.memset(ss, 0.0), W=[r_small])
            ACT(lambda h: h.activation(out=xn[0:SUBn, :], in_=xs[0:SUBn, :], func=AF.Square,
                                       accum_out=ss), R=[r_xs, r_small, TU], W=[r_xn, r_small])
            ACT(lambda h: h.activation(out=rs_, in_=ss, func=AF.Sqrt,
                                       bias=EPS_AP[0:SUBn, :], scale=1.0 / D), R=[r_small, r_const], W=[r_small])
            DVE(lambda h: h.reciprocal(out=rs_, in_=rs_), R=[r_small], W=[r_small])
            DVE(lambda h: h.tensor_scalar(out=xn[0:SUBn, :], in0=xs[0:SUBn, :], scalar1=rs_,
                                          scalar2=None, op0=ALU.mult), R=[r_xs, r_small, TU], W=[r_xn])
            for k0 in range(0, KC, 8):
                def ev(pv, br, k0=k0, st=st):
                    DVE(lambda h: h.tensor_tensor(out=hT[:, k0:k0 + 8, tl(st)], in0=pv,
                                                  in1=gT[:, l, k0:k0 + 8, None].to_broadcast([128, 8, SUBn]),
                                                  op=ALU.mult), R=[br, r_const], W=[r_hT])
                transposes(lambda j, k0=k0: xn[0:SUBn, (k0 + j) * 128:(k0 + j + 1) * 128], 8, 128, SUBn, ev,
                           [r_xn, TU])

        SP(ropeq[0:SUBn, 0:NT, :], rq_src[T.tok0:T.tok0 + TT, :].rearrange("(s p) c -> p s c", p=SUBn), W=[r_rope])
        SP(ropei[0:SUBn, 0:NT, :], ri_src[T.tok0:T.tok0 + TT, :].rearrange("(s p) c -> p s c", p=SUBn), W=[r_rope])

        deferred = []

        def run_deferred(keep=0):
            while len(deferred) > keep:
                deferred.pop(0)()

        def tm_block(bi, handler):
            w = min(512, c.TM_W - bi * 512)
            slot, rs = load_w(l, blk_tm(bi), w)
            for st in range(NT):
                b = mmbank()
                for kc in range(KC):
                    PE(lambda h, kc=kc, b=b, st=st: h.matmul(banks[b][0:SUBn, 0:w], lhsT=hT[:, kc, tl(st)],
                                                          rhs=slot[:, kc, 0:w], start=(kc == 0), stop=(kc == KC - 1)),
                       R=[r_hT, rs], W=[bres[b]], inc=(kc == KC - 1))
                nd0 = len(deferred)
                handler(st, banks[b], bres[b])
                run_deferred(keep=len(deferred) - nd0)

        def h_ikiw(st, ps, br):
            i = nxt("st", 2)
            ikdup, r_ikdup = ikdup2[i], r_ikdup2[i]
            cosi, sini = ropei[0:SUBn, st, 0:32], ropei[0:SUBn, st, 32:64]
            src4 = ps[0:SUBn, 0:64].rearrange("p (h t d) -> p h t d", h=1, t=2)
            dst4 = ikst[i][0:SUBn, :].rearrange("p (h t d) -> p h t d", h=1, t=2)
            rope(src4, cosi, sini, dst4, 1, 32, SUBn, [br], [r_ikst[i]])
            SP(ik_dst[rows(st), :], ikst[i][0:SUBn, :], R=[r_ikst[i]])
            DVE(lambda h: h.tensor_copy(out=ikdup[0:SUBn, :, :],
                                        in_=ikst[i][0:SUBn, None, :].to_broadcast([SUBn, 2, IDX_DIM])),
                R=[r_ikst[i]], W=[r_ikdup])
            ACT(lambda h: h.activation(out=absw[0:SUBn, st, :], in_=ps[0:SUBn, 64:80], func=AF.Abs),
                R=[br], W=[r_iw])
            ACT(lambda h: h.activation(out=sgn[0:SUBn, st, :], in_=ps[0:SUBn, 64:80], func=AF.Sign),
                R=[br], W=[r_iw])

            def ev(pv, br2):
                if T.kind == "p":
                    kt = (T.tok0 // 128) + st
                    ACT(lambda h: h.activation(out=IKT[:, kt * 128:(kt + 1) * 128], in_=pv[:, 0, :], func=AF.Copy),
                        R=[br2], W=[r_IKT])
                else:
                    ACT(lambda h: h.activation(out=iknew[:, st, :], in_=pv[:, 0, :], func=AF.Copy),
                        R=[br2], W=[r_new])
            deferred.append(lambda: transposes(lambda j: ikdup[0:SUBn, :, :].rearrange("p a d -> p (a d)"), 1, 128, SUBn, ev, [r_ikdup]))

        def h_iq(half_i):
            def hh(st, ps, br):
                i = nxt("tmf", 2)
                j = nxt("tmb", 2)
                cosi, sini = ropei[0:SUBn, st, 0:32], ropei[0:SUBn, st, 32:64]
                src4 = ps[0:SUBn, :].rearrange("p (h t d) -> p h t d", h=8, t=2)
                dst4 = tmf[i][0:SUBn, :].rearrange("p (h t d) -> p h t d", h=8, t=2)
                rope(src4, cosi, sini, dst4, 8, 32, SUBn, [br], [r_tmf[i]])
                DVE(lambda h: h.tensor_tensor(
                    out=tmb[j][0:SUBn, :].rearrange("p (h d) -> p h d", h=8),
                    in0=tmf[i][0:SUBn, :].rearrange("p (h d) -> p h d", h=8),
                    in1=absw[0:SUBn, st, half_i * 8:half_i * 8 + 8, None].to_broadcast([SUBn, 8, IDX_DIM]),
                    op=ALU.mult), R=[r_tmf[i], r_iw, TU], W=[r_tmb[j]])

                def ev(pv, br2):
                    ACT(lambda h: h.activation(out=IQT[:, half_i * 4:half_i * 4 + 4, tl(st)], in_=pv, func=AF.Copy),
                        R=[br2, TU], W=[r_IQT])
                deferred.append(lambda: transposes(lambda jj: tmb[j][0:SUBn, jj * 128:(jj + 1) * 128], 4, 128, SUBn, ev, [r_tmb[j], TU]))
            return hh

        def h_q(half_i):
            def hh(st, ps, br):
                j = nxt("tmb", 2)
                cosq, sinq = ropeq[0:SUBn, st, 0:64], ropeq[0:SUBn, st, 64:128]
                src4 = ps[0:SUBn, :].rearrange("p (h t d) -> p h t d", h=4, t=2)
                dst4 = tmb[j][0:SUBn, :].rearrange("p (h t d) -> p h t d", h=4, t=2)
                rope(src4, cosq, sinq, dst4, 4, 64, SUBn, [br], [r_tmb[j]])

                def ev(pv, br2):
                    ACT(lambda h: h.activation(out=QT[:, half_i * 4:half_i * 4 + 4, tl(st)], in_=pv, func=AF.Copy),
                        R=[br2, TU], W=[r_QT])
                deferred.append(lambda: transposes(lambda jj: tmb[j][0:SUBn, jj * 128:(jj + 1) * 128], 4, 128, SUBn, ev, [r_tmb[j], TU]))
            return hh

        def h_kv(st, ps, br):
            i = nxt("st", 2)
            j = nxt("tmb", 2)
            cosq, sinq = ropeq[0:SUBn, st, 0:64], ropeq[0:SUBn, st, 64:128]
            src4 = ps[0:SUBn, 0:256].rearrange("p (h t d) -> p h t d", h=2, t=2)
            dst4 = kst[i][0:SUBn, :].rearrange("p (h t d) -> p h t d", h=2, t=2)
            rope(src4, cosq, sinq, dst4, 2, 64, SUBn, [br], [r_kst[i]])
            SP(k_dst[rows(st), :], kst[i][0:SUBn, :], R=[r_kst[i], TU])
            ACT(lambda h: h.activation(out=tmb[j][0:SUBn, 0:256], in_=kst[i][0:SUBn, :], func=AF.Copy),
                R=[r_kst[i], TU], W=[r_tmb[j]])
            ACT(lambda h: h.activation(out=vst[i][0:SUBn, :], in_=ps[0:SUBn, 256:512], func=AF.Copy),
                R=[br, TU], W=[r_vst[i]])
            SP(v_dst[rows(st), :], vst[i][0:SUBn, :], R=[r_vst[i], TU])
            if T.kind == "p":
                kt = (T.tok0 // 128) + st
                ACT(lambda h: h.activation(out=V[0:SUBn, kt, :], in_=ps[0:SUBn, 256:512], func=AF.Copy),
                    R=[br], W=[r_V])
            else:
                ACT(lambda h: h.activation(out=vnew[0:SUBn, st, :], in_=ps[0:SUBn, 256:512], func=AF.Copy),
                    R=[br], W=[r_new])

            def ev(pv, br2):
                if T.kind == "p":
                    kt = (T.tok0 // 128) + st
                    ACT(lambda h: h.activation(out=KT[:, :, kt * 128:(kt + 1) * 128], in_=pv, func=AF.Copy),
                        R=[br2], W=[r_KT])
                else:
                    ACT(lambda h: h.activation(out=knew[:, st, :, :], in_=pv, func=AF.Copy), R=[br2], W=[r_new])
            deferred.append(lambda: transposes(lambda jj: tmb[j][0:SUBn, jj * 128:(jj + 1) * 128], 2, 128, SUBn, ev, [r_tmb[j], TU]))

        tm_block(5, h_ikiw)
        tm_block(3, h_iq(0))
        tm_block(4, h_iq(1))
        tm_block(0, h_q(0))
        tm_block(1, h_q(1))
        tm_block(2, h_kv)
        run_deferred()

        mark('L%d %s%d B' % (l, T.kind, T.idx))
        barrier()
        scale = float(HEAD_DIM) ** -0.5
        U8 = mybir.dt.uint8
        junk8 = U[:, Rbuf_off // 2: Rbuf_off // 2 + 3072].bitcast(U8)

        def nkeys_of(st):
            return (T.tok0 + (st + 1) * 128) if T.kind == "p" else (PAST + DEC)

        def idx(st):
            nkeys = nkeys_of(st)
            q_sl = tl(st)
            if T.kind == "s":
                load_sample_cache(l, st)
            DVE(lambda h: h.tensor_tensor(out=diag[0:SUBn, :, 0:SUBn],
                                           in0=ident[0:SUBn, None, 0:SUBn].to_broadcast([SUBn, N_IDX, SUBn]),
                                           in1=sgn[0:SUBn, st, :, None].to_broadcast([SUBn, N_IDX, SUBn]),
                                           op=ALU.mult), R=[r_iw, r_const, TU], W=[r_diag])
            for kb in range(0, nkeys, 512):
                w = min(512, nkeys - kb)
                sb_i = 4 + (kb // 512) % 2
                pend = None
                for pp in range(N_IDX // 2 + 1):
                    cur = None
                    if pp < N_IDX // 2:
                        pb = nxt("dots", 2)
                        cur = []
                        for half in range(2):
                            hd = 2 * pp + half
                            pr = slice(64 * half, 64 * half + 64)
                            db = [0, 1, 6, 7][2 * pb + half]
                            PE(lambda h: h.matmul(banks[db][0:SUBn, 0:w], lhsT=IQT[pr, pp, q_sl],
                                                  rhs=IKT[pr, kb:kb + w], start=True, stop=True),
                               R=[r_IQT, r_IKT, TU], W=[bres[db]])
                            cur.append((hd, db))
                    if pend is not None:
                        for ph, pri in pend:
                            PE(lambda h: h.matmul(banks[sb_i][0:SUBn, 0:w], lhsT=diag[0:SUBn, ph, 0:SUBn],
                                                  rhs=Rbuf[pri][0:SUBn, 0:w], start=(ph == 0),
                                                  stop=(ph == N_IDX - 1)),
                               R=[r_diag, r_R[pri], TU], W=[bres[sb_i]], inc=True)
                    pend = None
                    if cur is not None:
                        pend = []
                        for hd, db in cur:
                            ri = nxt("R", 6)
                            if hd % 2 == 0:
                                ACT(lambda h: h.activation(out=Rbuf[ri][0:SUBn, 0:w], in_=banks[db][0:SUBn, 0:w],
                                                           func=AF.Relu), R=[bres[db], TU], W=[r_R[ri]])
                            else:
                                DVE(lambda h: h.tensor_scalar_max(out=Rbuf[ri][0:SUBn, 0:w],
                                                                  in0=banks[db][0:SUBn, 0:w], scalar1=0.0),
                                    R=[bres[db], TU], W=[r_R[ri]])
                            pend.append((hd, ri))
                ACT(lambda h: h.activation(out=scores[0:SUBn, kb:kb + w], in_=banks[sb_i][0:SUBn, 0:w],
                                           func=AF.Copy), R=[bres[sb_i], TU], W=[r_scores])

        lo, mid, cn, sa, tt, stp, rmx, wh0 = [bs[0:SUBn, i:i + 1] for i in range(8)]
        r_lo, r_mid, r_cn, r_sa, r_tt, r_stp, r_rmx, r_wh0 = r_bs
        EPj = U[:, EP_off // 2: EP_off // 2 + 3072]

        def bisect_gen(st, use_act=False, pipelined=False):
            nkeys = nkeys_of(st)
            sc = scores[0:SUBn, 0:nkeys]
            DVE(lambda h: h.tensor_reduce(out=lo, in_=sc, axis=AX.X, op=ALU.min), R=[r_scores, TU], W=[r_lo])
            if T.kind == "p":
                DVE(lambda h: h.memset(scores[0:64, nkeys - 64:nkeys], NEG), R=[TU], W=[r_scores])
            if nkeys > topk:
                DVE(lambda h: h.tensor_reduce(out=rmx, in_=sc, axis=AX.X, op=ALU.max), R=[r_scores, TU], W=[r_rmx])
                DVE(lambda h: h.tensor_tensor(out=wh0, in0=rmx, in1=lo, op=ALU.subtract), R=[r_rmx, r_lo], W=[r_wh0])
                DVE(lambda h: h.tensor_scalar(out=whs[0:SUBn, :], in0=pow2[0:SUBn, :], scalar1=wh0, scalar2=None,
                                              op0=ALU.mult), R=[r_wh0, r_const], W=[r_whs])
                DVE(lambda h: h.tensor_tensor(out=mid, in0=lo, in1=whs[0:SUBn, 0:1], op=ALU.add),
                    R=[r_lo, r_whs], W=[r_mid])
                if use_act and pipelined:
                    nd = min(3072, max(64, (int(nkeys * 0.66) // 64) * 64))
                    na = nkeys - nd
                    assert na <= 1536
                    ajunk = U[:, (Rbuf_off + 3072) // 2: (Rbuf_off + 6144) // 2][0:SUBn, 0:na]
                    wd, wa = r_R[0:3], r_R[3:6]
                elif use_act:
                    nd = max(64, (nkeys // 2 // 64) * 64)
                    na = nkeys - nd
                    ajunk = EPj[0:SUBn, 0:na]
                    wd, wa = r_R, r_EP
                else:
                    nd, na = nkeys, 0
                    wd = r_R
                junk = junk8[0:SUBn, 0:nd]
                yield
                for it in range(NIT):
                    DVE(lambda h: h.tensor_scalar(out=junk, in0=scores[0:SUBn, 0:nd], scalar1=mid, scalar2=None,
                                                  op0=ALU.is_ge, op1=ALU.add, accum_out=cn),
                        R=[r_scores, r_mid, TU], W=wd + [r_cn])
                    if use_act:
                        ACT(lambda h: h.activation(out=ajunk, in_=scores[0:SUBn, nd:nkeys], func=AF.Sign,
                                                   bias=mid, scale=-1.0, accum_out=sa),
                            R=[r_scores, r_mid, TU], W=wa + [r_sa], drain=True)
                        DVE(lambda h: h.scalar_tensor_tensor(out=tt, in0=cn, scalar=2.0, in1=sa, op0=ALU.mult,
                                                             op1=ALU.subtract), R=[r_cn, r_sa], W=[r_tt])
                        DVE(lambda h: h.tensor_scalar(out=stp, in0=tt, scalar1=float(2 * topk - na),
                                                      scalar2=whs[0:SUBn, it:it + 1], op0=ALU.is_ge, op1=ALU.mult),
                            R=[r_tt, r_whs], W=[r_stp])
                    else:
                        DVE(lambda h: h.tensor_scalar(out=stp, in0=cn, scalar1=float(topk),
                                                      scalar2=whs[0:SUBn, it:it + 1], op0=ALU.is_ge, op1=ALU.mult),
                            R=[r_cn, r_whs], W=[r_stp])
                    last = (it == NIT - 1)
                    nx = it if last else it + 1
                    DVE(lambda h: h.scalar_tensor_tensor(out=(lo if last else mid), in0=stp,
                                                         scalar=whs[0:SUBn, nx:nx + 1], in1=mid,
                                                         op0=ALU.subtract, op1=ALU.add),
                        R=[r_stp, r_whs, r_mid], W=[r_lo if last else r_mid])
                    yield

        def bisect(st, use_act=False):
            for _ in bisect_gen(st, use_act=use_act):
                pass

        def mask(st):
            nkeys = nkeys_of(st)
            DVE(lambda h: h.tensor_scalar(out=mneg[0:SUBn, 0:nkeys], in0=scores[0:SUBn, 0:nkeys], scalar1=lo,
                                          scalar2=-30000.0, op0=ALU.is_lt, op1=ALU.mult),
                R=[r_scores, r_lo, TU], W=[r_mneg])

        def attn_main(st, gen=None):
            nkeys = nkeys_of(st)
            step = 0
            q_sl = tl(st)
            nq4 = 4 * SUBn
            nkt = (nkeys + 127) // 128
            i4 = I4p[:, :, :].rearrange("p a q -> p (a q)") if SUBn == 128 else \
                I4s[:, :, :].rearrange("p a q -> p (a q)")
            for g in range(2):
                ob, dbk = (4, 5) if g == 0 else (0, 1)
                pend = None
                for kt in range(nkt + 1):
                    if kt < nkt:
                        kw = min(128, nkeys - kt * 128)
                        sbk = 2 + nxt("sbk", 2)
                        PE(lambda h: h.matmul(banks[sbk][0:kw, 0:nq4], lhsT=KT[:, g, kt * 128:kt * 128 + kw],
                                              rhs=QT[:, 4 * g:4 * g + 4, q_sl], start=True, stop=False),
                           R=[r_KT, r_QT, TU], W=[bres[sbk]], inc=False)
                        PE(lambda h: h.matmul(banks[sbk][0:kw, 0:nq4], lhsT=mneg[0:SUBn, kt * 128:kt * 128 + kw],
                                              rhs=i4[0:SUBn, 0:nq4], start=False, stop=True),
                           R=[r_mneg, r_const, TU], W=[bres[sbk]], inc=True)
                        ei = nxt("EP", 6)
                        ACT(lambda h: h.activation(out=EP[ei][0:kw, 0:nq4], in_=banks[sbk][0:kw, 0:nq4],
                                                   func=AF.Exp, scale=scale), R=[bres[sbk], TU], W=[r_EP[ei]])
                        if gen is not None:
                            total = 2 * nkt
                            due = ((step + 1) * NIT) // total - (step * NIT) // total
                            for _ in range(due):
                                next(gen, None)
                            step += 1
                    if pend is not None:
                        pkt, pkw, pei = pend
                        PE(lambda h: h.matmul(banks[ob][:, 0:nq4], lhsT=V[0:pkw, pkt, g * 128:(g + 1) * 128],
                                              rhs=EP[pei][0:pkw, 0:nq4], start=(pkt == 0), stop=(pkt == nkt - 1)),
                           R=[r_V, r_EP[pei], TU], W=[bres[ob]], inc=False)
                        PE(lambda h: h.matmul(banks[dbk][:, 0:nq4], lhsT=ones[0:pkw, :], rhs=EP[pei][0:pkw, 0:nq4],
                                              start=(pkt == 0), stop=(pkt == nkt - 1)),
                           R=[r_const, r_EP[pei], TU], W=[bres[dbk]], inc=True)
                    pend = (kt, kw, ei) if kt < nkt else None

        def attn_epi(st):
            q_sl = tl(st)
            nq4 = 4 * SUBn
            for g in range(2):
                ob, dbk = (4, 5) if g == 0 else (0, 1)
                DVE(lambda h: h.reciprocal(out=rden[:, 0:nq4], in_=banks[dbk][:, 0:nq4]), R=[bres[dbk]], W=[r_rden])
                DVE(lambda h: h.tensor_tensor(
                    out=attnT[:, 4 * g:4 * g + 4, q_sl],
                    in0=banks[ob][:, 0:nq4].rearrange("p (a q) -> p a q", a=4),
                    in1=rden[:, 0:nq4].rearrange("p (a q) -> p a q", a=4), op=ALU.mult),
                    R=[bres[ob], r_rden], W=[r_attn])

        if T.kind == "p":
            idx(0)
            bisect(0, use_act=True)
            mask(0)
            for st in range(1, NT):
                idx(st)
                gen = bisect_gen(st, use_act=True, pipelined=True)
                next(gen, None)
                attn_main(st - 1, gen=gen)
                for _ in gen:
                    pass
                attn_epi(st - 1)
                mask(st)
            attn_main(NT - 1)
            attn_epi(NT - 1)
        else:
            for st in range(NT):
                idx(st)
                bisect(st, use_act=True)
                mask(st)
                attn_main(st)
                attn_epi(st)

        if l == 0:
            dump("dbg_attn_" + T.kind, attnT[:, :, 0:TT], [r_attn])
            dump("dbg_scores_" + T.kind, scores[0:SUBn, :], [r_scores, TU])
            dump("dbg_lo_" + T.kind, small[0:SUBn, :], [r_small])
        mark('L%d %s%d C4' % (l, T.kind, T.idx))
        barrier()
        SEGL, NSEG = T.SEGL, T.NSEG
        CL = 2 + SEGL
        PL = POOL_HIST + SEGL
        cin_v = cin[:, :, 0:NSEG * CL].rearrange("p c (s t) -> p c s t", s=NSEG)
        pbuf_v = pbuf[:, :, 0:NSEG * PL].rearrange("p c (s t) -> p c s t", s=NSEG)

        def fm_block(bi, handler):
            slot, rs = load_w(l, blk_fm(bi))
            for j in range(4):
                b = mmbank()
                for kc in range(KC):
                    PE(lambda h, kc=kc, b=b, j=j: h.matmul(banks[b][:, 0:TT], lhsT=slot[:, kc, j * 128:(j + 1) * 128],
                                                          rhs=hT[:, kc, 0:TT], start=(kc == 0), stop=(kc == KC - 1)),
                       R=[r_hT, rs], W=[bres[b]], inc=(kc == KC - 1))
                handler(j, banks[b][:, 0:TT], bres[b])

        def silu_to(ps, br):
            i = nxt("tmpf", 2)
            ACT(lambda h: h.activation(out=tmpf[i][:, 0:TT], in_=ps, func=AF.Silu), R=[br, TU], W=[r_tmpf[i]])
            return tmpf[i][:, 0:TT], r_tmpf[i]

        def h_gate_a(half_i):
            def hh(j, ps, br):
                hd = half_i * 4 + j
                sg, rsg = silu_to(ps, br)
                DVE(lambda h: h.tensor_tensor(out=y_a[:, hd, 0:TT], in0=attnT[:, hd, 0:TT], in1=sg, op=ALU.mult),
                    R=[r_attn, rsg, TU], W=[r_ya])
            return hh

        def h_u(j, ps, br):
            ACT(lambda h: h.activation(out=ubuf[:, j, 0:TT], in_=ps, func=AF.Copy), R=[br, TU], W=[r_ubuf])

        def h_cgate(j, ps, br):
            if T.kind == "p":
                DVE(lambda h: h.tensor_copy(out=cin_v[:, j, 0, 0:2], in_=hist_c[:, j, :]), R=[r_histc, TU], W=[r_cin])
            else:
                for s in range(NSEG):
                    SP(cin_v[:, j, s, 0:2], sconv[l, s, :, j * 128:(j + 1) * 128].rearrange("t p -> p t"),
                       R=[TU], W=[r_cin], slow=True)
            DVE(lambda h: h.tensor_tensor(out=cin_v[:, j, :, 2:CL],
                                          in0=ps.rearrange("p (s t) -> p s t", s=NSEG),
                                          in1=ubuf[:, j, 0:TT].rearrange("p (s t) -> p s t", s=NSEG), op=ALU.mult),
                R=[br, r_ubuf, TU], W=[r_cin])
            if T.kind == "p":
                DVE(lambda h: h.tensor_copy(out=hist_c[:, j, :], in_=cin_v[:, j, 0, CL - 2:CL]), R=[r_cin, TU],
                    W=[r_histc])
                if T.last:
                    SP(convp[l, :, j * 128:(j + 1) * 128].rearrange("t p -> p t"), cin_v[:, j, 0, CL - 2:CL],
                       R=[r_cin, TU], slow=True)
            else:
                for s in range(NSEG):
                    SP(convs[l, s, :, j * 128:(j + 1) * 128].rearrange("t p -> p t"), cin_v[:, j, s, CL - 2:CL],
                       R=[r_cin, TU], slow=True)
            uv = ubuf[:, j, 0:TT].rearrange("p (s t) -> p s t", s=NSEG)
            DVE(lambda h: h.tensor_scalar(out=uv, in0=cin_v[:, j, :, 0:SEGL], scalar1=cwT[:, l, 0, j:j + 1],
                                           scalar2=cbT[:, l, j:j + 1], op0=ALU.mult, op1=ALU.add),
                 R=[r_cin, r_const, TU], W=[r_ubuf])
            DVE(lambda h: h.scalar_tensor_tensor(out=uv, in0=cin_v[:, j, :, 1:SEGL + 1],
                                                  scalar=cwT[:, l, 1, j:j + 1], in1=uv, op0=ALU.mult, op1=ALU.add),
                 R=[r_cin, r_const, r_ubuf, TU], W=[r_ubuf])
            DVE(lambda h: h.scalar_tensor_tensor(out=uv, in0=cin_v[:, j, :, 2:SEGL + 2],
                                                  scalar=cwT[:, l, 2, j:j + 1], in1=uv, op0=ALU.mult, op1=ALU.add),
                 R=[r_cin, r_const, r_ubuf, TU], W=[r_ubuf])

        def h_bgate(j, ps, br):
            DVE(lambda h: h.tensor_tensor(out=ubuf[:, j, 0:TT], in0=ps, in1=ubuf[:, j, 0:TT], op=ALU.mult),
                R=[br, r_ubuf, TU], W=[r_ubuf])

        def h_gate_b(j, ps, br):
            sg, rsg = silu_to(ps, br)
            DVE(lambda h: h.tensor_tensor(out=y_b[:, j, 0:TT], in0=ubuf[:, j, 0:TT], in1=sg, op=ALU.mult),
                R=[r_ubuf, rsg, TU], W=[r_yb])

        def h_pin(j, ps, br):
            if T.kind == "p":
                DVE(lambda h: h.tensor_copy(out=pbuf_v[:, j, 0, 0:POOL_HIST], in_=hist_p[:, j, :]),
                    R=[r_histp, TU], W=[r_pbuf])
            else:
                for s in range(NSEG):
                    SP(pbuf_v[:, j, s, 0:POOL_HIST], spool[l, s, :, j * 128:(j + 1) * 128].rearrange("t p -> p t"),
                       R=[TU], W=[r_pbuf], slow=True)
            ACT(lambda h: h.activation(out=pbuf_v[:, j, :, POOL_HIST:PL],
                                       in_=ps.rearrange("p (s t) -> p s t", s=NSEG), func=AF.Copy),
                R=[br, TU], W=[r_pbuf])
            if T.kind == "p":
                DVE(lambda h: h.tensor_copy(out=hist_p[:, j, :], in_=pbuf_v[:, j, 0, PL - POOL_HIST:PL]),
                    R=[r_pbuf, TU], W=[r_histp])
                if T.last:
                    SP(poolp[l, :, j * 128:(j + 1) * 128].rearrange("t p -> p t"),
                       pbuf_v[:, j, 0, PL - POOL_HIST:PL], R=[r_pbuf, TU], slow=True)
            else:
                for s in range(NSEG):
                    SP(pools[l, s, :, j * 128:(j + 1) * 128].rearrange("t p -> p t"),
                       pbuf_v[:, j, s, PL - POOL_HIST:PL], R=[r_pbuf, TU], slow=True)
            wav = wa[:, 0:NSEG * PL].rearrange("p (s t) -> p s t", s=NSEG)
            wbv = wb[:, 0:NSEG * PL].rearrange("p (s t) -> p s t", s=NSEG)
            src, rsrc = pbuf_v[:, j, :, :], r_pbuf
            bufs = [(wav, r_wa), (wbv, r_wb)]
            lo_i = 0
            for lev in range(j + 1):
                sh = 1 << lev
                dstv, rdst = bufs[lev % 2]
                a0 = lo_i + sh
                DVE(lambda h, src=src, dstv=dstv, a0=a0, sh=sh: h.tensor_tensor(
                    out=dstv[:, :, a0:PL], in0=src[:, :, a0:PL], in1=src[:, :, a0 - sh:PL - sh], op=ALU.add),
                    R=[rsrc, TU], W=[rdst])
                src, rsrc = dstv, rdst
                lo_i = a0
            win = float(1 << (j + 1))
            di = nxt("dbf", 2)
            dv = dbf[di][:, 0:TT].rearrange("p (s t) -> p s t", s=NSEG)
            DVE(lambda h, src=src: h.scalar_tensor_tensor(out=dv, in0=src[:, :, POOL_HIST:PL], scalar=1.0 / win,
                                                          in1=pbuf_v[:, j, :, POOL_HIST:PL], op0=ALU.mult,
                                                          op1=ALU.subtract), R=[rsrc, r_pbuf, TU], W=[r_dbf[di]])
            if T.kind == "p" and T.first:
                t16 = t16buf[:, :]
                DVE(lambda h, src=src: h.tensor_tensor(out=t16, in0=src[:, 0, POOL_HIST:POOL_HIST + 16],
                                                       in1=invc[:, j, :], op=ALU.mult), R=[rsrc, r_const, TU],
                    W=[r_rtmp])
                DVE(lambda h: h.tensor_tensor(out=dbf[di][:, 0:16], in0=t16,
                                              in1=pbuf_v[:, j, 0, POOL_HIST:POOL_HIST + 16], op=ALU.subtract),
                    R=[r_rtmp, r_pbuf, TU], W=[r_dbf[di]])
            b = 6 + (j % 2)
            PE(lambda h: h.matmul(banks[b][:, 0:TT], lhsT=poolw[:, l, j, :], rhs=dbf[di][:, 0:TT], start=True,
                                  stop=True), R=[r_const, r_dbf[di], TU], W=[bres[b]])
            ACT(lambda h: h.activation(out=mixs[:, j, 0:TT], in_=banks[b][:, 0:TT], func=AF.Copy,
                                       scale=psT[:, l, j:j + 1]), R=[bres[b], r_const, TU], W=[r_mixs])

        def h_gate_c(j, ps, br):
            sg, rsg = silu_to(ps, br)
            DVE(lambda h: h.tensor_tensor(out=y_c[:, j, 0:TT], in0=mixs[:, j, 0:TT], in1=sg, op=ALU.mult),
                R=[r_mixs, rsg, TU], W=[r_yc])

        fm_block(0, h_gate_a(0))
        fm_block(1, h_gate_a(1))
        fm_block(2, h_u)
        fm_block(4, h_cgate)
        fm_block(3, h_bgate)
        fm_block(5, h_gate_b)
        barrier()
        fm_block(6, h_pin)
        fm_block(7, h_gate_c)

        if l == 0:
            dump("dbg_ya_" + T.kind, y_a[:, :, 0:TT], [r_ya, TU])
            dump("dbg_yb_" + T.kind, y_b[:, :, 0:TT], [r_yb, TU])
            dump("dbg_yc_" + T.kind, y_c[:, :, 0:TT], [r_yc, TU])
        mark('L%d %s%d C5' % (l, T.kind, T.idx))
        barrier()
        SP(gfin[0:SUBn, :], xsrc[rows(0), :], R=[TU, r_x1], W=[r_gfin])
        ysrc = [(y_a, r_ya, 0, 8), (y_b, r_yb, 8, 4), (y_c, r_yc, 12, 4)]
        for nb in range(c.ND):
            nxt("wslot", 4)
            lslot, lrs = load_w(l, blk_lift(nb))
            for bx in range(3):
                mslot, mrs = load_w(l, blk_fm(8 + bx * c.ND + nb))
                yv, ry, k0, nk = ysrc[bx]
                for j in range(4):
                    b = mmbank()
                    for kc in range(KC):
                        PE(lambda h, kc=kc, b=b, j=j: h.matmul(banks[b][:, 0:TT],
                                                              lhsT=mslot[:, kc, j * 128:(j + 1) * 128],
                                                              rhs=hT[:, kc, 0:TT], start=(kc == 0),
                                                              stop=(kc == KC - 1)),
                           R=[r_hT, mrs], W=[bres[b]], inc=(kc == KC - 1))
                    si = nxt("sig", 2)
                    ACT(lambda h, b=b, si=si: h.activation(out=sigb[si][:, 0:TT], in_=banks[b][:, 0:TT],
                                                           func=AF.Sigmoid), R=[bres[b], TU], W=[r_sig[si]])
                    b2 = trbank()
                    for kk in range(nk):
                        PE(lambda h, kk=kk, b2=b2, j=j: h.matmul(banks[b2][:, 0:TT],
                                                                lhsT=lslot[:, k0 + kk, j * 128:(j + 1) * 128],
                                                                rhs=yv[:, kk, 0:TT], start=(kk == 0),
                                                                stop=(kk == nk - 1)),
                           R=[ry, lrs, TU], W=[bres[b2]], inc=(kk == nk - 1))
                    if bx == 0:
                        DVE(lambda h, b2=b2, si=si, j=j: h.tensor_tensor(out=zacc[:, j, 0:TT], in0=banks[b2][:, 0:TT],
                                                                        in1=sigb[si][:, 0:TT], op=ALU.mult),
                            R=[bres[b2], r_sig[si], TU], W=[r_zacc])
                    else:
                        zi = nxt("zt", 2)
                        DVE(lambda h, b2=b2, si=si, zi=zi: h.tensor_tensor(out=ztmp[zi][:, 0:TT],
                                                                          in0=banks[b2][:, 0:TT],
                                                                          in1=sigb[si][:, 0:TT], op=ALU.mult),
                            R=[bres[b2], r_sig[si], TU], W=[r_ztmp[zi]])
                        if bx == 1:
                            DVE(lambda h, zi=zi, j=j: h.tensor_tensor(out=zacc[:, j, 0:TT], in0=zacc[:, j, 0:TT],
                                                                      in1=ztmp[zi][:, 0:TT], op=ALU.add),
                                 R=[r_ztmp[zi], r_zacc, TU], W=[r_zacc])
                        else:
                            DVE(lambda h, zi=zi, j=j: h.tensor_tensor(out=zT[:, nb * 4 + j, 0:TT],
                                                                      in0=zacc[:, j, 0:TT], in1=ztmp[zi][:, 0:TT],
                                                                      op=ALU.add),
                                 R=[r_ztmp[zi], r_zacc, TU], W=[r_zT])

        if l == 0:
            dump("dbg_z_" + T.kind, zT[:, :, 0:TT], [r_zT, TU])
        mark('L%d %s%d C6' % (l, T.kind, T.idx))
        barrier()
        xr = gfin
        for nb in range(c.ND):
            slot, rs = load_w(l, blk_out(nb))
            for st in range(NT):
                b = nxt("c6", 8)
                for kc in range(KC):
                    PE(lambda h, kc=kc, b=b, st=st: h.matmul(banks[b][0:SUBn, :], lhsT=zT[:, kc, tl(st)],
                                                          rhs=slot[:, kc, :], start=(kc == 0), stop=(kc == KC - 1)),
                       R=[r_zT, rs, TU], W=[bres[b]], inc=(kc == KC - 1))
                ACT(lambda h, b=b, st=st, nb=nb: h.activation(out=xo[0:SUBn, st, nb * 512:(nb + 1) * 512],
                                                             in_=banks[b][0:SUBn, :], func=AF.Copy),
                    R=[bres[b], TU], W=[r_xo[st]])
        hstage = [hT[:, 0:8, :].rearrange("p a b -> p (a b)").bitcast(F32),
                  hT[:, 8:16, :].rearrange("p a b -> p (a b)").bitcast(F32)]
        stg = {}
        for st in range(NT):
            if st == 0:
                stg[st] = (xr, r_gfin)
            elif st in (1, 2) and KC == 16:
                stg[st] = (hstage[st - 1], r_hT)
                SP(hstage[st - 1][0:SUBn, :], xsrc[rows(st), :], R=[r_x1], W=[r_hT])
            else:
                stg[st] = (xr, r_gfin)
        for st in range(NT):
            buf, rb = stg[st]
            if st > 0 and buf is xr:
                SP(xr[0:SUBn, :], xsrc[rows(st), :], R=[TU, r_x1], W=[r_gfin])
            DVE(lambda h, st=st, buf=buf: h.tensor_tensor(out=xo[0:SUBn, st, :], in0=xo[0:SUBn, st, :],
                                                         in1=buf[0:SUBn, :], op=ALU.add),
                R=[rb, r_xo[st], TU], W=[r_xo[st]])
            if not last_layer:
                SP(xdst[rows(st), :], xo[0:SUBn, st, :], R=[r_xo[st], TU], W=[r_x1])
        if last_layer:
            SP(gfin[:, :], fng.partition_broadcast(128), R=[TU], W=[r_gfin])
            for st in range(NT):
                DVE(lambda h: h.memset(small[0:SUBn, 10:11], 0.0), W=[r_small])
                ACT(lambda h, st=st: h.activation(out=hT[0:SUBn, 0:4, :].rearrange("p a b -> p (a b)")[:, 0:D],
                                                 in_=xo[0:SUBn, st, :], func=AF.Square,
                                                 accum_out=small[0:SUBn, 10:11]), R=[r_xo[st], r_small, TU],
                    W=[r_hT, r_small])
                ACT(lambda h: h.activation(out=small[0:SUBn, 11:12], in_=small[0:SUBn, 10:11], func=AF.Sqrt,
                                           bias=EPS_AP[0:SUBn, :], scale=1.0 / D), R=[r_small, r_const], W=[r_small])
                DVE(lambda h: h.reciprocal(out=small[0:SUBn, 11:12], in_=small[0:SUBn, 11:12]), R=[r_small],
                    W=[r_small])
                DVE(lambda h, st=st: h.scalar_tensor_tensor(out=xo[0:SUBn, st, :], in0=xo[0:SUBn, st, :],
                                                           scalar=small[0:SUBn, 11:12], in1=gfin[0:SUBn, :],
                                                           op0=ALU.mult, op1=ALU.mult),
                    R=[r_xo[st], r_small, r_gfin, TU], W=[r_xo[st]])
                SP(xdst[rows(st), :], xo[0:SUBn, st, :], R=[r_xo[st], TU])

    def load_sample_cache(l, b):
        npast_t = PAST // 128
        GDMA(V[:, 0:npast_t, :], cv[l, b].rearrange("(t p) c -> p t c", p=128), W=[r_V])
        for r0 in range(0, npast_t, 8):
            nr = min(8, npast_t - r0)
            GDMA(kstg[:, 0:nr, :], ck[l, b, r0 * 128:(r0 + nr) * 128, :].rearrange("(t p) c -> p t c", p=128),
                 R=[TU], W=[r_kstg])
            for dup in range(2):
                GDMA(ikstg[:, 0:nr, dup, :],
                     cik[l, b, r0 * 128:(r0 + nr) * 128, :].rearrange("(t p) c -> p t c", p=128), R=[TU], W=[r_ikstg])
            for g in range(2):
                def ev(pv, br2, g=g, r0=r0, nr=nr):
                    ACT(lambda h: h.activation(out=KT[:, g, r0 * 128:(r0 + nr) * 128].rearrange(
                        "p (j t) -> p j t", j=nr), in_=pv, func=AF.Copy), R=[br2], W=[r_KT])
                transposes(lambda jj, g=g: kstg[:, jj, g * 128:(g + 1) * 128], nr, 128, 128, ev, [r_kstg])

            def ev3(pv, br2, r0=r0, nr=nr):
                ACT(lambda h: h.activation(out=IKT[:, r0 * 128:(r0 + nr) * 128].rearrange("p (j t) -> p j t", j=nr),
                                           in_=pv, func=AF.Copy), R=[br2], W=[r_IKT])
            transposes(lambda jj: ikstg[:, jj, :, :].rearrange("p a d -> p (a d)"), nr, 128, 128, ev3, [r_ikstg])
        DVE(lambda h: h.tensor_copy(out=KT[:, :, PAST:PAST + DEC], in_=knew[:, b, :, :]), R=[r_new], W=[r_KT])
        DVE(lambda h: h.tensor_copy(out=IKT[:, PAST:PAST + DEC], in_=iknew[:, b, :]), R=[r_new], W=[r_IKT])
        DVE(lambda h: h.tensor_copy(out=V[0:DEC, npast_t, :], in_=vnew[0:DEC, b, :]), R=[r_new], W=[r_V])

    EPS_AP = sb("eps_ap", [128, 1], F32)
    DVE(lambda h: h.memset(EPS_AP[:, :], EPS), W=[r_const])

    for l in range(DEPTH):
        for T in tiles:
            process(l, T)

    mark('END')
    S.finish(sp)
    S.finish(pool)

    sem_handles = [es.enter_context(nc.semaphore(n)) for n in S.sems]
    block = es.enter_context(nc.Block())

    def replay(e, h):
        for item in e.prog:
            if item[0] == "wait":
                h.wait_ge(sem_handles[item[1]], item[2])
            else:
                _, call, sk, incv = item
                ins = getattr(h, call[0])(*call[1], **call[2])
                if sk is not None:
                    ins.then_inc(sem_handles[sk], incv)

    @block.tensor
    def _(h):
        replay(pe, h)

    @block.scalar
    def _(h):
        replay(act, h)

    @block.vector
    def _(h):
        replay(dve, h)

    @block.gpsimd
    def _(h):
        replay(pool, h)

    @block.sync
    def _(h):
        replay(sp, h)

    es.close()
    return nc


def _rope_tables(pos, half):
    inv = (np.float32(THETA) ** (-np.arange(half, dtype=np.float32) / np.float32(half))).astype(np.float32)
    ang = pos.astype(np.float32)[:, None] * inv[None, :]
    return np.concatenate([np.cos(ang), np.sin(ang)], axis=1).astype(np.float32)


def make_consts(cfg):
    c = cfg
    pos_p = np.arange(c.SEQ)
    pos_s = c.PAST + np.arange(c.DEC)
    pos_s2 = np.concatenate([pos_s, pos_s])
    invc = np.zeros((128, 4, 16), np.float32)
    for g in range(4):
        w = 2 << g
        invc[:, g, :] = 1.0 / np.minimum(w, np.arange(16) + 1).astype(np.float32)
    pow2 = np.tile((0.5 ** (np.arange(NIT) + 1)).astype(np.float32)[None, :], (128, 1))
    return {
        "ident": np.eye(128, dtype=np.float32),
        "ropeq_p": _rope_tables(pos_p, 64), "ropei_p": _rope_tables(pos_p, 32),
        "ropeq_s": _rope_tables(pos_s2, 64), "ropei_s": _rope_tables(pos_s2, 32),
        "invc": invc, "pow2": pow2,
    }


def make_in_maps(cfg, n_cores, inp):
    c = cfg
    consts = make_consts(c)
    f = lambda a: np.ascontiguousarray(np.asarray(a, dtype=np.float32))
    shared = {k: f(inp[k]) for k in ("norm_g", "w_in", "conv_w", "conv_b", "pool_w", "pool_scale", "lift_a",
                                     "lift_b", "lift_c", "w_out")}
    shared["fng"] = f(inp["final_norm_g"])
    shared.update(consts)
    maps = []
    for i in range(n_cores):
        m = dict(shared)
        m["xp"] = f(inp["x_prompt"][i])
        sl = slice(2 * i, 2 * i + 2)
        m["xs"] = f(np.asarray(inp["x_sample"])[sl].reshape(2 * c.DEC, c.D))
        m["ck"] = f(np.asarray(inp["cache_k"])[:, sl].reshape(c.DEPTH, 2, c.PAST, KV_W))
        m["cv"] = f(np.asarray(inp["cache_v"])[:, sl].reshape(c.DEPTH, 2, c.PAST, KV_W))
        m["cik"] = f(np.asarray(inp["cache_idx_k"])[:, sl])
        m["sconv"] = f(np.asarray(inp["state_conv"])[:, sl])
        m["spool"] = f(np.asarray(inp["state_pool"])[:, sl])
        maps.append(m)
    return maps


def gather(cfg, n_cores, results):
    c = cfg
    R = results
    cat = lambda k, ax: np.stack([np.asarray(r[k]) for r in R], axis=ax)
    y_prompt = cat("yp", 0)
    y_sample = np.concatenate([np.asarray(r["ys"]).reshape(2, c.DEC, c.D) for r in R], axis=0)
    k_prompt = cat("kp", 1).reshape(c.DEPTH, n_cores, c.SEQ, 2, HEAD_DIM)
    v_prompt = cat("vp", 1).reshape(c.DEPTH, n_cores, c.SEQ, 2, HEAD_DIM)
    idxk_prompt = cat("ikp", 1)
    conv_prompt = cat("convp", 1)
    pool_prompt = cat("poolp", 1)
    catb = lambda k, shp: np.concatenate([np.asarray(r[k]).reshape((c.DEPTH, 2) + shp) for r in R], axis=1)
    k_sample = catb("ks", (c.DEC, 2, HEAD_DIM))
    v_sample = catb("vs", (c.DEC, 2, HEAD_DIM))
    idxk_sample = catb("iks", (c.DEC, IDX_DIM))
    conv_sample = catb("convs", (2, CONV_W))
    pool_sample = catb("pools", (POOL_HIST, POOL_W))
    outs = (y_prompt, y_sample, k_prompt, v_prompt, idxk_prompt, conv_prompt, pool_prompt,
            k_sample, v_sample, idxk_sample, conv_sample, pool_sample)
    return tuple(np.ascontiguousarray(o.astype(np.float32)) for o in outs)


def run(cfg, n_cores, inp):
    nc = build(cfg)
    maps = make_in_maps(cfg, n_cores, inp)
    res = run_bass_kernel_spmd(nc, maps, core_ids=list(range(n_cores)))
    return gather(cfg, n_cores, res.results)


def kernel(**inputs):
    return run(Cfg(), N_CORES, inputs)
```

```python
import contextlib
import numpy as np
import concourse.bass as bass
import concourse.mybir as mybir
from concourse.bass_utils import run_bass_kernel_spmd

F32 = mybir.dt.float32
BF16 = mybir.dt.bfloat16
ALU = mybir.AluOpType
AF = mybir.ActivationFunctionType
AX = mybir.AxisListType

N_CORES = 8
CHUNK = 64
HEAD_DIM = 128
ATTN_W = 1024
KV_W = 256
IDX_DIM = 64
N_IDX = 16
CONV_W = 512
POOL_W = 512
POOL_HIST = 15
MAX_TOPK = 256
EPS = 1e-6
THETA = 10000.0
NIT = 16
NEG = -1.0e30


class Cfg:
    def __init__(self, D=2048, SEQ=4096, PAST=4096, DEC=64, DEPTH=2, debug=False):
        self.debug = debug
        self.D, self.SEQ, self.PAST, self.DEC, self.DEPTH = D, SEQ, PAST, DEC, DEPTH
        self.KC = D // 128
        self.TM_W = ATTN_W + 2 * KV_W + N_IDX * IDX_DIM + IDX_DIM + N_IDX
        self.FM_W = ATTN_W + 4 * CONV_W + 2 * POOL_W + 3 * D
        self.IN_W = self.TM_W + self.FM_W
        assert self.FM_W % 512 == 0 and D % 512 == 0
        self.NFM = self.FM_W // 512
        self.ND = D // 512
        self.NTM = (self.TM_W + 511) // 512
        self.NBLK = self.NTM + self.NFM + 2 * self.ND
        self.TOPK_P = min(MAX_TOPK, SEQ // 4)
        self.TOPK_S = min(MAX_TOPK, (PAST + DEC) // 4)
        self.KMAX = max(SEQ, PAST + DEC)
        self.NKT = (self.KMAX + 127) // 128


class Res:
    __slots__ = ("w", "r", "name", "excl")

    def __init__(self, name="", excl=False):
        self.w = None
        self.r = {}
        self.name = name
        self.excl = excl


class Eng:
    def __init__(self, name, semkey, is_pe=False):
        self.name, self.semkey, self.is_pe = name, semkey, is_pe
        self.cnt = 0
        self.seen = {}
        self.prog = []
        self.ring = []
        self.ndma = 0


class _Rec:
    def __init__(self):
        self.call = None

    def __getattr__(self, name):
        def f(*a, **k):
            self.call = (name, a, k)
            return self
        return f


def _capture(fn):
    r = _Rec()
    fn(r)
    assert r.call is not None
    return r.call


class Sched:
    def __init__(self):
        self.sems = []
        self.engs = {}

    def new_sem(self, name):
        self.sems.append(name)
        return len(self.sems) - 1

    def engine(self, name, is_pe=False, ring=0):
        e = Eng(name, self.new_sem("s_" + name), is_pe)
        e.ring = [self.new_sem("r_%s%d" % (name, i)) for i in range(ring)]
        self.engs[name] = e
        return e

    def _deps(self, e, R, W):
        deps = {}

        def need(tok, raw):
            if tok is None:
                return
            sk, v = tok
            if sk == e.semkey and e.is_pe:
                return
            if deps.get(sk, 0) < v:
                deps[sk] = v
        for r in R:
            need(r.w, True)
        for w in W:
            need(w.w, False)
            for sk, v in w.r.items():
                need((sk, v), False)
        for sk, v in deps.items():
            if e.seen.get(sk, 0) < v:
                e.prog.append(("wait", sk, v))
                e.seen[sk] = v

    def _mark(self, tok, R, W):
        for r in R:
            if r.r.get(tok[0], 0) < tok[1]:
                r.r[tok[0]] = tok[1]
        for w in W:
            w.w = tok
            w.r = {}

    def op(self, e, fn, R=(), W=(), inc=True, drain=False):
        pd = getattr(e, "pending_drain", 0)
        if pd:
            if e.seen.get(e.semkey, 0) < pd:
                e.prog.append(("wait", e.semkey, pd))
                e.seen[e.semkey] = pd
            e.pending_drain = 0
        if drain:
            e.pending_drain = e.cnt + 1
        if any(r.excl for r in R):
            W = list(W) + [r for r in R if r.excl]
            R = [r for r in R if not r.excl]
        self._deps(e, R, W)
        tok = (e.semkey, e.cnt + 1)
        e.prog.append(("op", _capture(fn), e.semkey if inc else None, 1))
        if inc:
            e.cnt += 1
        self._mark(tok, R, W)

    def dma(self, e, fn, R=(), W=()):
        n = e.ndma
        e.ndma += 1
        G = len(e.ring)
        sk = e.ring[n % G]
        prev = 16 * (n // G)
        if prev > 0 and e.seen.get(sk, 0) < prev:
            e.prog.append(("wait", sk, prev))
            e.seen[sk] = prev
        self._deps(e, R, W)
        tok = (sk, prev + 16)
        e.prog.append(("op", _capture(fn), sk, 16))
        self._mark(tok, R, W)

    def finish(self, e):
        for i, sk in enumerate(e.ring):
            n_on = (e.ndma - i + len(e.ring) - 1) // len(e.ring) if e.ndma > i else 0
            if n_on > 0:
                e.prog.append(("wait", sk, 16 * n_on))


def build(cfg):
    c = cfg
    D, KC, SEQ, PAST, DEC, DEPTH = c.D, c.KC, c.SEQ, c.PAST, c.DEC, c.DEPTH
    KMAX, NKT = c.KMAX, c.NKT
    nc = bass.Bass("TRN2", target_bir_lowering=False)

    def din(name, shape, dt=F32):
        return nc.dram_tensor(name, list(shape), dt, kind="ExternalInput").ap()

    def dout(name, shape):
        return nc.dram_tensor(name, list(shape), F32, kind="ExternalOutput").ap()

    def dint(name, shape, dt):
        return nc.dram_tensor(name, list(shape), dt, kind="Internal").ap()

    xp = din("xp", [SEQ, D])
    xs_in = din("xs", [2 * DEC, D])
    ck = din("ck", [DEPTH, 2, PAST, KV_W])
    cv = din("cv", [DEPTH, 2, PAST, KV_W])
    cik = din("cik", [DEPTH, 2, PAST, IDX_DIM])
    sconv = din("sconv", [DEPTH, 2, 2, CONV_W])
    spool = din("spool", [DEPTH, 2, POOL_HIST, POOL_W])
    norm_g = din("norm_g", [DEPTH, D])
    w_in = din("w_in", [DEPTH, D, c.IN_W])
    conv_w = din("conv_w", [DEPTH, 3, CONV_W])
    conv_b = din("conv_b", [DEPTH, CONV_W])
    pool_w = din("pool_w", [DEPTH, 4, 128, 128])
    pool_scale = din("pool_scale", [DEPTH, POOL_W])
    lift_a = din("lift_a", [DEPTH, ATTN_W, D])
    lift_b = din("lift_b", [DEPTH, CONV_W, D])
    lift_c = din("lift_c", [DEPTH, POOL_W, D])
    w_out = din("w_out", [DEPTH, D, D])
    fng = din("fng", [D])
    ident_in = din("ident", [128, 128])
    ropeq_p = din("ropeq_p", [SEQ, 128])
    ropei_p = din("ropei_p", [SEQ, 64])
    ropeq_s = din("ropeq_s", [2 * DEC, 128])
    ropei_s = din("ropei_s", [2 * DEC, 64])
    invc_in = din("invc", [128, 4, 16])
    pow2_in = din("pow2", [128, NIT])

    yp = dout("yp", [SEQ, D])
    ys = dout("ys", [2 * DEC, D])
    kp = dout("kp", [DEPTH, SEQ, KV_W])
    vp = dout("vp", [DEPTH, SEQ, KV_W])
    ikp = dout("ikp", [DEPTH, SEQ, IDX_DIM])
    convp = dout("convp", [DEPTH, 2, CONV_W])
    poolp = dout("poolp", [DEPTH, POOL_HIST, POOL_W])
    ks = dout("ks", [DEPTH, 2 * DEC, KV_W])
    vs = dout("vs", [DEPTH, 2 * DEC, KV_W])
    iks = dout("iks", [DEPTH, 2 * DEC, IDX_DIM])
    convs = dout("convs", [DEPTH, 2, 2, CONV_W])
    pools = dout("pools", [DEPTH, 2, POOL_HIST, POOL_W])

    wsc = dint("wsc", [DEPTH, c.NBLK, 128, KC * 512], BF16)
    if c.debug:
        x1p = dout("x1p", [SEQ, D])
        x1s = dout("x1s", [2 * DEC, D])
    else:
        x1p = dint("x1p", [SEQ, D], F32)
        x1s = dint("x1s", [2 * DEC, D], F32)
    dbg_names = []

    def dump(name, ap, R):
        if not c.debug:
            return
        d = dout(name, list(ap.shape))
        dbg_names.append(name)
        GDMA(d, ap, R=R)

    S = Sched()
    marks = []
    build.marks = marks

    def mark(label):
        marks.append((label, sum(1 for x in pe.prog if x[0] == 'op')))
    pe = S.engine("pe", is_pe=True)
    act = S.engine("act")
    dve = S.engine("dve")
    pool = S.engine("pool", ring=8)
    sp = S.engine("sp", ring=8)

    es = contextlib.ExitStack()

    def sb(name, shape, dt):
        return es.enter_context(nc.sbuf_tensor("t_" + name, list(shape), dt))

    TTMAX = 512
    ident_f = sb("ident_f", [128, 128], F32)
    ident = sb("ident", [128, 128], BF16)
    ones = sb("ones", [128, 128], BF16)
    gT = sb("gT", [128, DEPTH, KC], F32)
    cwT = sb("cwT", [128, DEPTH, 3, 4], F32)
    cbT = sb("cbT", [128, DEPTH, 4], F32)
    psT = sb("psT", [128, DEPTH, 4], F32)
    poolw = sb("poolw", [128, DEPTH, 4, 128], BF16)
    invc = sb("invc_sb", [128, 4, 16], F32)
    pow2 = sb("pow2_sb", [128, NIT], F32)
    KT = sb("KT", [128, 2, KMAX], BF16)
    V = sb("V", [128, NKT, KV_W], BF16)
    IKT = sb("IKT", [128, KMAX], BF16)
    hT = sb("hT", [128, KC, TTMAX], BF16)
    attnT = sb("attnT", [128, 8, TTMAX], BF16)
    wslot = [sb("wslot%d" % i, [128, KC, 512], BF16) for i in range(4)]
    ropeq = sb("ropeq", [128, 4, 128], F32)
    ropei = sb("ropei", [128, 4, 64], F32)
    ikst = [sb("ikst%d" % i, [128, IDX_DIM], F32) for i in range(2)]
    ikdup2 = [sb("ikdup%d" % i, [128, 2, IDX_DIM], BF16) for i in range(2)]
    absw = sb("absw", [128, 4, N_IDX], F32)
    sgn = sb("sgn", [128, 4, N_IDX], F32)
    t16buf = sb("t16buf", [128, 16], F32)
    small = sb("small", [128, 16], F32)
    bar_t = sb("bar_t", [128, 2], F32)
    whs = sb("whs", [128, NIT], F32)
    bs = sb("bs", [128, 8], F32)
    rden = sb("rden", [128, 512], F32)
    hist_c = sb("hist_c", [128, 4, 2], F32)
    hist_p = sb("hist_p", [128, 4, POOL_HIST], F32)
    knew = sb("knew", [128, 2, 2, 64], BF16)
    vnew = sb("vnew", [64, 2, KV_W], BF16)
    iknew = sb("iknew", [128, 2, 64], BF16)
    mneg = sb("mneg", [128, KMAX], BF16)
    I4p = sb("I4p", [128, 4, 128], BF16)
    I4s = sb("I4s", [64, 4, 64], BF16)
    UB = 57344
    U = sb("U", [128, UB // 2], BF16)

    class Carve:
        def __init__(self):
            self.off = 0

        def take(self, shape, dt):
            n = int(np.prod(shape[1:]))
            esz = 4 if dt == F32 else 2
            assert self.off % 4 == 0
            a = U[:, self.off // 2: self.off // 2 + n * esz // 2]
            self.off += n * esz
            assert self.off <= UB, (self.off, UB)
            if dt == F32:
                a = a.bitcast(F32)
            if len(shape) > 2:
                names = " ".join("d%d" % i for i in range(len(shape) - 1))
                kw = {"d%d" % i: shape[i + 1] for i in range(len(shape) - 1)}
                a = a.rearrange("p (%s) -> p %s" % (names, names), **kw)
            return a

    cb_ = Carve()
    QT = cb_.take([128, 8, TTMAX], BF16)
    IQT = cb_.take([128, 8, TTMAX], BF16)
    diag = cb_.take([128, N_IDX, 128], BF16)
    ab_off = cb_.off
    scores = cb_.take([128, KMAX], F32)
    Rbuf_off = cb_.off
    Rbuf = [cb_.take([128, 512], BF16) for _ in range(6)]
    EP_off = cb_.off
    EP = [cb_.take([128, 512], BF16) for _ in range(6)]
    kstg = cb_.take([128, 8, KV_W], BF16)
    ikstg = cb_.take([128, 8, 2, IDX_DIM], BF16)
    ca_ = Carve()
    ca_.off = ab_off
    xsb = [ca_.take([128, D], F32) for _ in range(2)]
    xnb = [ca_.take([128, D], BF16) for _ in range(2)]
    tmf = [ca_.take([128, 512], F32) for _ in range(2)]
    tmb = [ca_.take([128, 512], BF16) for _ in range(2)]
    rtmp = [ca_.take([128, 256], F32) for _ in range(2)]
    kst = [ca_.take([128, KV_W], F32) for _ in range(2)]
    vst = [ca_.take([128, KV_W], F32) for _ in range(2)]
    cc_ = Carve()
    zT = cc_.take([128, KC, TTMAX], BF16)
    y_a = cc_.take([128, 8, TTMAX], BF16)
    y_b = cc_.take([128, 4, TTMAX], BF16)
    y_c = cc_.take([128, 4, TTMAX], BF16)
    c4_off = cc_.off
    tmpf = [cc_.take([128, POOL_HIST + TTMAX + 1], F32) for _ in range(2)]
    dbf = [cc_.take([128, TTMAX], BF16) for _ in range(2)]
    c4b_off = cc_.off
    ubuf = cc_.take([128, 4, TTMAX], F32)
    cin = cc_.take([128, 4, 2 + TTMAX], F32)
    cc_.off = c4b_off
    pbuf = cc_.take([128, 4, POOL_HIST + TTMAX + 1], F32)
    mixs = cc_.take([128, 4, TTMAX], F32)
    wa, wb = tmpf[0], tmpf[1]
    cc_.off = c4_off
    zacc = cc_.take([128, 4, TTMAX], F32)
    sigb = [cc_.take([128, TTMAX], F32) for _ in range(2)]
    ztmp = [cc_.take([128, TTMAX], F32) for _ in range(2)]
    cc_.off = 16384
    xo = cc_.take([128, 4, D], F32)
    gfin = cc_.take([128, D], F32)

    banks = [es.enter_context(nc.psum_tensor("bank%d" % i, [128, 512], F32)) for i in range(8)]
    bres = [Res("bank%d" % i, excl=True) for i in range(8)]

    TU = Res("TU")
    r_const = Res("const")
    r_KT, r_V, r_IKT = Res("KT"), Res("V"), Res("IKT")
    r_hT = Res("hT")
    r_attn = Res("attnT")
    r_w = [Res("w%d" % i) for i in range(4)]
    r_rope = Res("rope")
    r_tmf = [Res(), Res()]
    r_tmb = [Res(), Res()]
    r_rtmp = Res()
    r_kst = [Res(), Res()]
    r_vst = [Res(), Res()]
    r_ikst = [Res(), Res()]
    r_ikdup2 = [Res(), Res()]
    r_iw = Res()
    r_small = Res()
    r_whs = Res()
    r_bs = [Res() for _ in range(8)]
    r_rden = Res()
    r_histc, r_histp = Res(), Res()
    r_new = Res()
    r_kstg, r_ikstg = Res(), Res()
    r_mneg = Res()
    r_scores = Res()
    r_R = [Res() for _ in range(6)]
    r_EP = [Res() for _ in range(6)]
    r_QT, r_IQT, r_diag = Res(), Res(), Res()
    r_xsb, r_xnb = [Res(), Res()], [Res(), Res()]
    r_zT, r_ya, r_yb, r_yc = Res(), Res(), Res(), Res()
    r_ubuf, r_cin, r_pbuf, r_mixs = Res(), Res(), Res(), Res()
    r_tmpf = [Res(), Res()]
    r_wa, r_wb = r_tmpf[0], r_tmpf[1]
    r_gfin = Res()
    r_dbf = [Res(), Res()]
    r_zacc = Res()
    r_sig = [Res(), Res()]
    r_ztmp = [Res(), Res()]
    r_xo = [Res() for _ in range(4)]
    r_wsc = [[Res() for _ in range(c.NBLK)] for _ in range(DEPTH)]
    r_x1 = Res()

    cnt = {"tmf": 0, "tmb": 0, "st": 0, "R": 0, "EP": 0, "tmpf": 0, "dbf": 0, "sig": 0, "zt": 0,
           "mm": 0, "tr": 0, "wslot": 0, "dots": 0, "sbk": 0, "c6": 0}

    def nxt(key, n):
        v = cnt[key] % n
        cnt[key] += 1
        return v

    cnt["mm4"] = 0
    cnt["tr4"] = 0

    def mmbank():
        return [0, 1, 4, 5][nxt("mm4", 4)]

    def trbank():
        return [2, 3, 6, 7][nxt("tr4", 4)]

    def barrier():
        S.op(dve, lambda h: h.memset(bar_t[0:1, 0:1], 0.0), W=[TU])

    def DVE(fn, R=(), W=()):
        S.op(dve, fn, R=R, W=W)

    def ACT(fn, R=(), W=(), drain=False):
        S.op(act, fn, R=R, W=W, drain=drain)

    def POOL(fn, R=(), W=()):
        S.op(pool, fn, R=R, W=W)

    def PE(fn, R=(), W=(), inc=True):
        S.op(pe, fn, R=R, W=W, inc=inc)

    def SP(out, in_, R=(), W=(), slow=False):
        if slow:
            S.dma(sp, lambda h: h.dma_start(out=out, in_=in_, allow_slow_non_contiguous=True), R=R, W=W)
        else:
            S.dma(sp, lambda h: h.dma_start(out=out, in_=in_), R=R, W=W)

    def GDMA(out, in_, R=(), W=()):
        S.dma(pool, lambda h: h.dma_start(out=out, in_=in_), R=R, W=W)

    def bank_bf(b):
        return banks[b][:, :].bitcast(BF16)

    def blk_tm(i): return i
    def blk_fm(i): return c.NTM + i
    def blk_lift(i): return c.NTM + c.NFM + i
    def blk_out(i): return c.NTM + c.NFM + c.ND + i

    def wsc_view(l, b):
        return wsc[l, b].rearrange("p (k n) -> p k n", k=KC)

    r_liftp = [[[Res() for _ in range(3)] for _ in range(c.ND)] for _ in range(DEPTH)]

    def convert_layer2(l):
        def wsrc(mat, nk, n0, w):
            return mat[0:nk * 128, n0:n0 + w].rearrange("(k p) n -> p k n", p=128)
        for i in range(c.NTM):
            n0 = i * 512
            w = min(512, c.TM_W - n0)
            GDMA(wsc_view(l, blk_tm(i))[:, :, 0:w], wsrc(w_in[l], KC, n0, w), W=[r_wsc[l][blk_tm(i)]])
        for i in range(c.NFM):
            n0 = c.TM_W + i * 512
            GDMA(wsc_view(l, blk_fm(i)), wsrc(w_in[l], KC, n0, 512), W=[r_wsc[l][blk_fm(i)]])
        for i in range(c.ND):
            n0 = i * 512
            v = wsc_view(l, blk_lift(i))
            GDMA(v[:, 0:8, :], wsrc(lift_a[l], 8, n0, 512), W=[r_liftp[l][i][0]])
            GDMA(v[:, 8:12, :], wsrc(lift_b[l], 4, n0, 512), W=[r_liftp[l][i][1]])
            GDMA(v[:, 12:16, :], wsrc(lift_c[l], 4, n0, 512), W=[r_liftp[l][i][2]])
        for i in range(c.ND):
            n0 = i * 512
            GDMA(wsc_view(l, blk_out(i)), wsrc(w_out[l], KC, n0, 512), W=[r_wsc[l][blk_out(i)]])

    def load_w(l, b, w=512, slot=None):
        s = nxt("wslot", 4) if slot is None else slot
        Rr = [r_wsc[l][b]]
        if c.NTM + c.NFM <= b < c.NTM + c.NFM + c.ND:
            Rr = r_liftp[l][b - c.NTM - c.NFM]
        SP(wslot[s][:, :, 0:w], wsc_view(l, b)[:, :, 0:w], R=Rr, W=[r_w[s]])
        return wslot[s], r_w[s]

    SP(ident_f[:, :], ident_in, W=[r_const])
    DVE(lambda h: h.tensor_copy(out=ident[:, :], in_=ident_f[:, :]), R=[r_const], W=[r_const])
    DVE(lambda h: h.memset(ones[:, :], 1.0), W=[r_const])
    DVE(lambda h: h.tensor_copy(out=I4p[:, :, :], in_=ident[:, None, :].to_broadcast([128, 4, 128])), R=[r_const], W=[r_const])
    DVE(lambda h: h.tensor_copy(out=I4s[:, :, :], in_=ident[0:64, None, 0:64].to_broadcast([64, 4, 64])), R=[r_const], W=[r_const])
    DVE(lambda h: h.memset(small[:, :], 0.0), W=[r_small])
    SP(invc[:, :, :], invc_in, W=[r_const])
    SP(pow2[:, :], pow2_in, W=[r_const])
    for l in range(DEPTH):
        SP(gT[:, l, :], norm_g[l].rearrange("(k p) -> p k", p=128), W=[r_const], slow=True)
        SP(cbT[:, l, :], conv_b[l].rearrange("(k p) -> p k", p=128), W=[r_const], slow=True)
        SP(psT[:, l, :], pool_scale[l].rearrange("(k p) -> p k", p=128), W=[r_const], slow=True)
        for j in range(3):
            SP(cwT[:, l, j, :], conv_w[l, j].rearrange("(k p) -> p k", p=128), W=[r_const], slow=True)
        GDMA(poolw[:, l, :, :], pool_w[l].rearrange("g c d -> c g d"), W=[r_const])
    for l in range(DEPTH):
        convert_layer2(l)

    class Tile:
        pass

    def prompt_tile(t):
        T = Tile()
        T.kind = "p"
        T.idx = t
        T.SUB = 128
        T.NT = min(4, SEQ // 128 - 4 * t)
        T.TT = T.NT * 128
        T.tok0 = t * 512
        T.NSEG = 1
        T.SEGL = T.TT
        T.first = (t == 0)
        T.last = (T.tok0 + T.TT == SEQ)
        return T

    def sample_tile():
        T = Tile()
        T.kind = "s"
        T.idx = 0
        T.SUB = DEC
        T.NT = 2
        T.TT = 2 * DEC
        T.tok0 = 0
        T.NSEG = 2
        T.SEGL = DEC
        T.first = False
        T.last = True
        return T

    tiles = [prompt_tile(t) for t in range((SEQ + 511) // 512)] + [sample_tile()]

    def rope(src4, cos, sin, dst4, H, half, SUBn, Rsrc, Wdst):
        x1, x2 = src4[:, :, 0, :], src4[:, :, 1, :]
        cb = cos[:, None, :].to_broadcast([SUBn, H, half])
        sb_ = sin[:, None, :].to_broadcast([SUBn, H, half])
        n = H * half
        t1 = rtmp[0][0:SUBn, 0:n].rearrange("p (h d) -> p h d", h=H)
        t2 = rtmp[1][0:SUBn, 0:n].rearrange("p (h d) -> p h d", h=H)
        DVE(lambda h: h.tensor_tensor(out=t1, in0=x1, in1=cb, op=ALU.mult), R=Rsrc + [r_rope, TU], W=[r_rtmp])
        DVE(lambda h: h.tensor_tensor(out=t2, in0=x2, in1=sb_, op=ALU.mult), R=Rsrc + [r_rope, TU], W=[r_rtmp])
        DVE(lambda h: h.tensor_tensor(out=dst4[:, :, 0, :], in0=t1, in1=t2, op=ALU.subtract), R=[r_rtmp, TU], W=Wdst)
        DVE(lambda h: h.tensor_tensor(out=t1, in0=x2, in1=cb, op=ALU.mult), R=Rsrc + [r_rope, TU], W=[r_rtmp])
        DVE(lambda h: h.tensor_tensor(out=t2, in0=x1, in1=sb_, op=ALU.mult), R=Rsrc + [r_rope, TU], W=[r_rtmp])
        DVE(lambda h: h.tensor_tensor(out=dst4[:, :, 1, :], in0=t1, in1=t2, op=ALU.add), R=[r_rtmp, TU], W=Wdst)

    def transposes(src_fn, n, rows, SUBn, evac_fn, Rsrc):
        b = trbank()
        pv = bank_bf(b)[:, 0:n * SUBn].rearrange("p (j t) -> p j t", j=n)
        for j in range(n):
            srcj = src_fn(j)
            PE(lambda h, srcj=srcj, j=j: h.transpose(out=pv[0:rows, j, :], in_=srcj, identity=ident[0:SUBn, 0:SUBn]),
               R=Rsrc + [r_const], W=[bres[b]], inc=(j == n - 1))
        evac_fn(pv, bres[b])

    def process(l, T):
        SUBn, NT, TT = T.SUB, T.NT, T.TT
        last_layer = (l == DEPTH - 1)
        if T.kind == "p":
            xsrc = xp if l == 0 else x1p
            xdst = yp if last_layer else x1p
            rq_src, ri_src = ropeq_p, ropei_p
            k_dst, v_dst, ik_dst = kp[l], vp[l], ikp[l]
            topk = c.TOPK_P
        else:
            xsrc = xs_in if l == 0 else x1s
            xdst = ys if last_layer else x1s
            rq_src, ri_src = ropeq_s, ropei_s
            k_dst, v_dst, ik_dst = ks[l], vs[l], iks[l]
            topk = c.TOPK_S

        def rows(st):
            return slice(T.tok0 + st * SUBn, T.tok0 + (st + 1) * SUBn)

        def tl(st):
            return slice(st * SUBn, (st + 1) * SUBn)

        mark('L%d %s%d A' % (l, T.kind, T.idx))
        barrier()
        if T.kind == "p" and T.first:
            DVE(lambda h: h.memset(hist_c[:, :, :], 0.0), W=[r_histc])
            DVE(lambda h: h.memset(hist_p[:, :, :], 0.0), W=[r_histp])
        for st in range(NT):
            xs, r_xs = xsb[st % 2], r_xsb[st % 2]
            xn, r_xn = xnb[st % 2], r_xnb[st % 2]
            ss, rs_ = small[0:SUBn, 12 + (st % 2):13 + (st % 2)], small[0:SUBn, 0 + (st % 2):1 + (st % 2)]
            SP(xs[0:SUBn, :], xsrc[rows(st), :], R=[TU, r_x1], W=[r_xs])
            DVE(lambda h: h.memset(ss, 0.0), W=[r_small])
            ACT(lambda h: h.activation(out=xn[0:SUBn, :], in_=xs[0:SUBn, :], func=AF.Square,
                                       accum_out=ss), R=[r_xs, r_small, TU], W=[r_xn, r_small])
            ACT(lambda h: h.activation(out=rs_, in_=ss, func=AF.Sqrt,
                                       bias=EPS_AP[0:SUBn, :], scale=1.0 / D), R=[r_small, r_const], W=[r_small])
            DVE(lambda h: h.reciprocal(out=rs_, in_=rs_), R=[r_small], W=[r_small])
            DVE(lambda h: h.tensor_scalar(out=xn[0:SUBn, :], in0=xs[0:SUBn, :], scalar1=rs_,
                                          scalar2=None, op0=ALU.mult), R=[r_xs, r_small, TU], W=[r_xn])
            for k0 in range(0, KC, 8):
                def ev(pv, br, k0=k0, st=st):
                    DVE(lambda h: h.tensor_tensor(out=hT[:, k0:k0 + 8, tl(st)], in0=pv,
                                                  in1=gT[:, l, k0:k0 + 8, None].to_broadcast([128, 8, SUBn]),
                                                  op=ALU.mult), R=[br, r_const], W=[r_hT])
                transposes(lambda j, k0=k0: xn[0:SUBn, (k0 + j) * 128:(k0 + j + 1) * 128], 8, 128, SUBn, ev,
                           [r_xn, TU])

        SP(ropeq[0:SUBn, 0:NT, :], rq_src[T.tok0:T.tok0 + TT, :].rearrange("(s p) c -> p s c", p=SUBn), W=[r_rope])
        SP(ropei[0:SUBn, 0:NT, :], ri_src[T.tok0:T.tok0 + TT, :].rearrange("(s p) c -> p s c", p=SUBn), W=[r_rope])

        deferred = []

        def run_deferred(keep=0):
            while len(deferred) > keep:
                deferred.pop(0)()

        def tm_block(bi, handler):
            w = min(512, c.TM_W - bi * 512)
            slot, rs = load_w(l, blk_tm(bi), w)
            for st in range(NT):
                b = mmbank()
                for kc in range(KC):
                    PE(lambda h, kc=kc, b=b, st=st: h.matmul(banks[b][0:SUBn, 0:w], lhsT=hT[:, kc, tl(st)],
                                                          rhs=slot[:, kc, 0:w], start=(kc == 0), stop=(kc == KC - 1)),
                       R=[r_hT, rs], W=[bres[b]], inc=(kc == KC - 1))
                nd0 = len(deferred)
                handler(st, banks[b], bres[b])
                run_deferred(keep=len(deferred) - nd0)

        def h_ikiw(st, ps, br):
            i = nxt("st", 2)
            ikdup, r_ikdup = ikdup2[i], r_ikdup2[i]
            cosi, sini = ropei[0:SUBn, st, 0:32], ropei[0:SUBn, st, 32:64]
            src4 = ps[0:SUBn, 0:64].rearrange("p (h t d) -> p h t d", h=1, t=2)
            dst4 = ikst[i][0:SUBn, :].rearrange("p (h t d) -> p h t d", h=1, t=2)
            rope(src4, cosi, sini, dst4, 1, 32, SUBn, [br], [r_ikst[i]])
            SP(ik_dst[rows(st), :], ikst[i][0:SUBn, :], R=[r_ikst[i]])
            DVE(lambda h: h.tensor_copy(out=ikdup[0:SUBn, :, :],
                                        in_=ikst[i][0:SUBn, None, :].to_broadcast([SUBn, 2, IDX_DIM])),
                R=[r_ikst[i]], W=[r_ikdup])
            ACT(lambda h: h.activation(out=absw[0:SUBn, st, :], in_=ps[0:SUBn, 64:80], func=AF.Abs),
                R=[br], W=[r_iw])
            ACT(lambda h: h.activation(out=sgn[0:SUBn, st, :], in_=ps[0:SUBn, 64:80], func=AF.Sign),
                R=[br], W=[r_iw])

            def ev(pv, br2):
                if T.kind == "p":
                    kt = (T.tok0 // 128) + st
                    ACT(lambda h: h.activation(out=IKT[:, kt * 128:(kt + 1) * 128], in_=pv[:, 0, :], func=AF.Copy),
                        R=[br2], W=[r_IKT])
                else:
                    ACT(lambda h: h.activation(out=iknew[:, st, :], in_=pv[:, 0, :], func=AF.Copy),
                        R=[br2], W=[r_new])
            deferred.append(lambda: transposes(lambda j: ikdup[0:SUBn, :, :].rearrange("p a d -> p (a d)"), 1, 128, SUBn, ev, [r_ikdup]))

        def h_iq(half_i):
            def hh(st, ps, br):
                i = nxt("tmf", 2)
                j = nxt("tmb", 2)
                cosi, sini = ropei[0:SUBn, st, 0:32], ropei[0:SUBn, st, 32:64]
                src4 = ps[0:SUBn, :].rearrange("p (h t d) -> p h t d", h=8, t=2)
                dst4 = tmf[i][0:SUBn, :].rearrange("p (h t d) -> p h t d", h=8, t=2)
                rope(src4, cosi, sini, dst4, 8, 32, SUBn, [br], [r_tmf[i]])
                DVE(lambda h: h.tensor_tensor(
                    out=tmb[j][0:SUBn, :].rearrange("p (h d) -> p h d", h=8),
                    in0=tmf[i][0:SUBn, :].rearrange("p (h d) -> p h d", h=8),
                    in1=absw[0:SUBn, st, half_i * 8:half_i * 8 + 8, None].to_broadcast([SUBn, 8, IDX_DIM]),
                    op=ALU.mult), R=[r_tmf[i], r_iw, TU], W=[r_tmb[j]])

                def ev(pv, br2):
                    ACT(lambda h: h.activation(out=IQT[:, half_i * 4:half_i * 4 + 4, tl(st)], in_=pv, func=AF.Copy),
                        R=[br2, TU], W=[r_IQT])
                deferred.append(lambda: transposes(lambda jj: tmb[j][0:SUBn, jj * 128:(jj + 1) * 128], 4, 128, SUBn, ev, [r_tmb[j], TU]))
            return hh

        def h_q(half_i):
            def hh(st, ps, br):
                j = nxt("tmb", 2)
                cosq, sinq = ropeq[0:SUBn, st, 0:64], ropeq[0:SUBn, st, 64:128]
                src4 = ps[0:SUBn, :].rearrange("p (h t d) -> p h t d", h=4, t=2)
                dst4 = tmb[j][0:SUBn, :].rearrange("p (h t d) -> p h t d", h=4, t=2)
                rope(src4, cosq, sinq, dst4, 4, 64, SUBn, [br], [r_tmb[j]])

                def ev(pv, br2):
                    ACT(lambda h: h.activation(out=QT[:, half_i * 4:half_i * 4 + 4, tl(st)], in_=pv, func=AF.Copy),
                        R=[br2, TU], W=[r_QT])
                deferred.append(lambda: transposes(lambda jj: tmb[j][0:SUBn, jj * 128:(jj + 1) * 128], 4, 128, SUBn, ev, [r_tmb[j], TU]))
            return hh

        def h_kv(st, ps, br):
            i = nxt("st", 2)
            j = nxt("tmb", 2)
            cosq, sinq = ropeq[0:SUBn, st, 0:64], ropeq[0:SUBn, st, 64:128]
            src4 = ps[0:SUBn, 0:256].rearrange("p (h t d) -> p h t d", h=2, t=2)
            dst4 = kst[i][0:SUBn, :].rearrange("p (h t d) -> p h t d", h=2, t=2)
            rope(src4, cosq, sinq, dst4, 2, 64, SUBn, [br], [r_kst[i]])
            SP(k_dst[rows(st), :], kst[i][0:SUBn, :], R=[r_kst[i], TU])
            ACT(lambda h: h.activation(out=tmb[j][0:SUBn, 0:256], in_=kst[i][0:SUBn, :], func=AF.Copy),
                R=[r_kst[i], TU], W=[r_tmb[j]])
            ACT(lambda h: h.activation(out=vst[i][0:SUBn, :], in_=ps[0:SUBn, 256:512], func=AF.Copy),
                R=[br, TU], W=[r_vst[i]])
            SP(v_dst[rows(st), :], vst[i][0:SUBn, :], R=[r_vst[i], TU])
            if T.kind == "p":
                kt = (T.tok0 // 128) + st
                ACT(lambda h: h.activation(out=V[0:SUBn, kt, :], in_=ps[0:SUBn, 256:512], func=AF.Copy),
                    R=[br], W=[r_V])
            else:
                ACT(lambda h: h.activation(out=vnew[0:SUBn, st, :], in_=ps[0:SUBn, 256:512], func=AF.Copy),
                    R=[br], W=[r_new])

            def ev(pv, br2):
                if T.kind == "p":
                    kt = (T.tok0 // 128) + st
                    ACT(lambda h: h.activation(out=KT[:, :, kt * 128:(kt + 1) * 128], in_=pv, func=AF.Copy),
                        R=[br2], W=[r_KT])
                else:
                    ACT(lambda h: h.activation(out=knew[:, st, :, :], in_=pv, func=AF.Copy), R=[br2], W=[r_new])
            deferred.append(lambda: transposes(lambda jj: tmb[j][0:SUBn, jj * 128:(jj + 1) * 128], 2, 128, SUBn, ev, [r_tmb[j], TU]))

        tm_block(5, h_ikiw)
        tm_block(3, h_iq(0))
        tm_block(4, h_iq(1))
        tm_block(0, h_q(0))
        tm_block(1, h_q(1))
        tm_block(2, h_kv)
        run_deferred()

        mark('L%d %s%d B' % (l, T.kind, T.idx))
        barrier()
        scale = float(HEAD_DIM) ** -0.5
        U8 = mybir.dt.uint8
        junk8 = U[:, Rbuf_off // 2: Rbuf_off // 2 + 3072].bitcast(U8)

        def nkeys_of(st):
            return (T.tok0 + (st + 1) * 128) if T.kind == "p" else (PAST + DEC)

        def all_selected(st):
            return T.kind == "p" and nkeys_of(st) <= topk

        def idx(st):
            nkeys = nkeys_of(st)
            q_sl = tl(st)
            if all_selected(st):
                return
            if T.kind == "s":
                load_sample_cache(l, st)
            DVE(lambda h: h.tensor_tensor(out=diag[0:SUBn, :, 0:SUBn],
                                           in0=ident[0:SUBn, None, 0:SUBn].to_broadcast([SUBn, N_IDX, SUBn]),
                                           in1=sgn[0:SUBn, st, :, None].to_broadcast([SUBn, N_IDX, SUBn]),
                                           op=ALU.mult), R=[r_iw, r_const, TU], W=[r_diag])
            for kb in range(0, nkeys, 512):
                w = min(512, nkeys - kb)
                sb_i = 4 + (kb // 512) % 2
                pend = None
                for pp in range(N_IDX // 2 + 1):
                    cur = None
                    if pp < N_IDX // 2:
                        pb = nxt("dots", 2)
                        cur = []
                        for half in range(2):
                            hd = 2 * pp + half
                            pr = slice(64 * half, 64 * half + 64)
                            db = [0, 1, 6, 7][2 * pb + half]
                            PE(lambda h: h.matmul(banks[db][0:SUBn, 0:w], lhsT=IQT[pr, pp, q_sl],
                                                  rhs=IKT[pr, kb:kb + w], start=True, stop=True),
                               R=[r_IQT, r_IKT, TU], W=[bres[db]])
                            cur.append((hd, db))
                    if pend is not None:
                        for ph, pri in pend:
                            PE(lambda h: h.matmul(banks[sb_i][0:SUBn, 0:w], lhsT=diag[0:SUBn, ph, 0:SUBn],
                                                  rhs=Rbuf[pri][0:SUBn, 0:w], start=(ph == 0),
                                                  stop=(ph == N_IDX - 1)),
                               R=[r_diag, r_R[pri], TU], W=[bres[sb_i]], inc=True)
                    pend = None
                    if cur is not None:
                        pend = []
                        for hd, db in cur:
                            ri = nxt("R", 6)
                            if hd % 2 == 0:
                                ACT(lambda h: h.activation(out=Rbuf[ri][0:SUBn, 0:w], in_=banks[db][0:SUBn, 0:w],
                                                           func=AF.Relu), R=[bres[db], TU], W=[r_R[ri]])
                            else:
                                DVE(lambda h: h.tensor_scalar_max(out=Rbuf[ri][0:SUBn, 0:w],
                                                                  in0=banks[db][0:SUBn, 0:w], scalar1=0.0),
                                    R=[bres[db], TU], W=[r_R[ri]])
                            pend.append((hd, ri))
                ACT(lambda h: h.activation(out=scores[0:SUBn, kb:kb + w], in_=banks[sb_i][0:SUBn, 0:w],
                                           func=AF.Copy), R=[bres[sb_i], TU], W=[r_scores])

        lo, mid, cn, sa, tt, stp, rmx, wh0 = [bs[0:SUBn, i:i + 1] for i in range(8)]
        r_lo, r_mid, r_cn, r_sa, r_tt, r_stp, r_rmx, r_wh0 = r_bs
        EPj = U[:, EP_off // 2: EP_off // 2 + 3072]

        def bisect_gen(st, use_act=False, pipelined=False):
            nkeys = nkeys_of(st)
            if all_selected(st):
                return
            sc = scores[0:SUBn, 0:nkeys]
            DVE(lambda h: h.tensor_reduce(out=lo, in_=sc, axis=AX.X, op=ALU.min), R=[r_scores, TU], W=[r_lo])
            if T.kind == "p":
                DVE(lambda h: h.memset(scores[0:64, nkeys - 64:nkeys], NEG), R=[TU], W=[r_scores])
            if nkeys > topk:
                DVE(lambda h: h.tensor_reduce(out=rmx, in_=sc, axis=AX.X, op=ALU.max), R=[r_scores, TU], W=[r_rmx])
                DVE(lambda h: h.tensor_tensor(out=wh0, in0=rmx, in1=lo, op=ALU.subtract), R=[r_rmx, r_lo], W=[r_wh0])
                DVE(lambda h: h.tensor_scalar(out=whs[0:SUBn, :], in0=pow2[0:SUBn, :], scalar1=wh0, scalar2=None,
                                              op0=ALU.mult), R=[r_wh0, r_const], W=[r_whs])
                DVE(lambda h: h.tensor_tensor(out=mid, in0=lo, in1=whs[0:SUBn, 0:1], op=ALU.add),
                    R=[r_lo, r_whs], W=[r_mid])
                if use_act and pipelined:
                    nd = min(3072, max(64, (int(nkeys * 0.66) // 64) * 64))
                    na = nkeys - nd
                    assert na <= 1536
                    ajunk = U[:, (Rbuf_off + 3072) // 2: (Rbuf_off + 6144) // 2][0:SUBn, 0:na]
                    wd, wa = r_R[0:3], r_R[3:6]
                elif use_act:
                    nd = max(64, (nkeys // 2 // 64) * 64)
                    na = nkeys - nd
                    ajunk = EPj[0:SUBn, 0:na]
                    wd, wa = r_R, r_EP
                else:
                    nd, na = nkeys, 0
                    wd = r_R
                junk = junk8[0:SUBn, 0:nd]
                yield
                for it in range(NIT):
                    DVE(lambda h: h.tensor_scalar(out=junk, in0=scores[0:SUBn, 0:nd], scalar1=mid, scalar2=None,
                                                  op0=ALU.is_ge, op1=ALU.add, accum_out=cn),
                        R=[r_scores, r_mid, TU], W=wd + [r_cn])
                    if use_act:
                        ACT(lambda h: h.activation(out=ajunk, in_=scores[0:SUBn, nd:nkeys], func=AF.Sign,
                                                   bias=mid, scale=-1.0, accum_out=sa),
                            R=[r_scores, r_mid, TU], W=wa + [r_sa], drain=True)
                        DVE(lambda h: h.scalar_tensor_tensor(out=tt, in0=cn, scalar=2.0, in1=sa, op0=ALU.mult,
                                                             op1=ALU.subtract), R=[r_cn, r_sa], W=[r_tt])
                        DVE(lambda h: h.tensor_scalar(out=stp, in0=tt, scalar1=float(2 * topk - na),
                                                      scalar2=whs[0:SUBn, it:it + 1], op0=ALU.is_ge, op1=ALU.mult),
                            R=[r_tt, r_whs], W=[r_stp])
                    else:
                        DVE(lambda h: h.tensor_scalar(out=stp, in0=cn, scalar1=float(topk),
                                                      scalar2=whs[0:SUBn, it:it + 1], op0=ALU.is_ge, op1=ALU.mult),
                            R=[r_cn, r_whs], W=[r_stp])
                    last = (it == NIT - 1)
                    nx = it if last else it + 1
                    DVE(lambda h: h.scalar_tensor_tensor(out=(lo if last else mid), in0=stp,
                                                         scalar=whs[0:SUBn, nx:nx + 1], in1=mid,
                                                         op0=ALU.subtract, op1=ALU.add),
                        R=[r_stp, r_whs, r_mid], W=[r_lo if last else r_mid])
                    yield

        def bisect(st, use_act=False):
            for _ in bisect_gen(st, use_act=use_act):
                pass

        def mask(st):
            nkeys = nkeys_of(st)
            if all_selected(st):
                DVE(lambda h: h.memset(mneg[0:SUBn, 0:nkeys], 0.0), R=[TU], W=[r_mneg])
                DVE(lambda h: h.memset(mneg[0:64, nkeys - 64:nkeys], -30000.0), R=[TU], W=[r_mneg])
                return
            DVE(lambda h: h.tensor_scalar(out=mneg[0:SUBn, 0:nkeys], in0=scores[0:SUBn, 0:nkeys], scalar1=lo,
                                          scalar2=-30000.0, op0=ALU.is_lt, op1=ALU.mult),
                R=[r_scores, r_lo, TU], W=[r_mneg])

        def attn_main(st, gen=None):
            nkeys = nkeys_of(st)
            step = 0
            q_sl = tl(st)
            nq4 = 4 * SUBn
            nkt = (nkeys + 127) // 128
            i4 = I4p[:, :, :].rearrange("p a q -> p (a q)") if SUBn == 128 else \
                I4s[:, :, :].rearrange("p a q -> p (a q)")
            for g in range(2):
                ob, dbk = (4, 5) if g == 0 else (0, 1)
                pend = None
                for kt in range(nkt + 1):
                    if kt < nkt:
                        kw = min(128, nkeys - kt * 128)
                        sbk = 2 + nxt("sbk", 2)
                        PE(lambda h: h.matmul(banks[sbk][0:kw, 0:nq4], lhsT=KT[:, g, kt * 128:kt * 128 + kw],
                                              rhs=QT[:, 4 * g:4 * g + 4, q_sl], start=True, stop=False),
                           R=[r_KT, r_QT, TU], W=[bres[sbk]], inc=False)
                        PE(lambda h: h.matmul(banks[sbk][0:kw, 0:nq4], lhsT=mneg[0:SUBn, kt * 128:kt * 128 + kw],
                                              rhs=i4[0:SUBn, 0:nq4], start=False, stop=True),
                           R=[r_mneg, r_const, TU], W=[bres[sbk]], inc=True)
                        ei = nxt("EP", 6)
                        ACT(lambda h: h.activation(out=EP[ei][0:kw, 0:nq4], in_=banks[sbk][0:kw, 0:nq4],
                                                   func=AF.Exp, scale=scale), R=[bres[sbk], TU], W=[r_EP[ei]])
                        if gen is not None:
                            total = 2 * nkt
                            due = ((step + 1) * NIT) // total - (step * NIT) // total
                            for _ in range(due):
                                next(gen, None)
                            step += 1
                    if pend is not None:
                        pkt, pkw, pei = pend
                        PE(lambda h: h.matmul(banks[ob][:, 0:nq4], lhsT=V[0:pkw, pkt, g * 128:(g + 1) * 128],
                                              rhs=EP[pei][0:pkw, 0:nq4], start=(pkt == 0), stop=(pkt == nkt - 1)),
                           R=[r_V, r_EP[pei], TU], W=[bres[ob]], inc=False)
                        PE(lambda h: h.matmul(banks[dbk][:, 0:nq4], lhsT=ones[0:pkw, :], rhs=EP[pei][0:pkw, 0:nq4],
                                              start=(pkt == 0), stop=(pkt == nkt - 1)),
                           R=[r_const, r_EP[pei], TU], W=[bres[dbk]], inc=True)
                    pend = (kt, kw, ei) if kt < nkt else None

        def attn_epi(st):
            q_sl = tl(st)
            nq4 = 4 * SUBn
            for g in range(2):
                ob, dbk = (4, 5) if g == 0 else (0, 1)
                DVE(lambda h: h.reciprocal(out=rden[:, 0:nq4], in_=banks[dbk][:, 0:nq4]), R=[bres[dbk]], W=[r_rden])
                DVE(lambda h: h.tensor_tensor(
                    out=attnT[:, 4 * g:4 * g + 4, q_sl],
                    in0=banks[ob][:, 0:nq4].rearrange("p (a q) -> p a q", a=4),
                    in1=rden[:, 0:nq4].rearrange("p (a q) -> p a q", a=4), op=ALU.mult),
                    R=[bres[ob], r_rden], W=[r_attn])

        if T.kind == "p":
            idx(0)
            bisect(0, use_act=True)
            mask(0)
            for st in range(1, NT):
                idx(st)
                gen = bisect_gen(st, use_act=True, pipelined=True)
                next(gen, None)
                attn_main(st - 1, gen=gen)
                for _ in gen:
                    pass
                attn_epi(st - 1)
                mask(st)
            attn_main(NT - 1)
            attn_epi(NT - 1)
        else:
            for st in range(NT):
                idx(st)
                bisect(st, use_act=True)
                mask(st)
                attn_main(st)
                attn_epi(st)

        if l == 0:
            dump("dbg_attn_" + T.kind, attnT[:, :, 0:TT], [r_attn])
            dump("dbg_scores_" + T.kind, scores[0:SUBn, :], [r_scores, TU])
            dump("dbg_lo_" + T.kind, small[0:SUBn, :], [r_small])
        mark('L%d %s%d C4' % (l, T.kind, T.idx))
        barrier()
        SEGL, NSEG = T.SEGL, T.NSEG
        CL = 2 + SEGL
        PL = POOL_HIST + SEGL
        cin_v = cin[:, :, 0:NSEG * CL].rearrange("p c (s t) -> p c s t", s=NSEG)
        pbuf_v = pbuf[:, :, 0:NSEG * PL].rearrange("p c (s t) -> p c s t", s=NSEG)

        def fm_block(bi, handler):
            slot, rs = load_w(l, blk_fm(bi))
            for j in range(4):
                b = mmbank()
                for kc in range(KC):
                    PE(lambda h, kc=kc, b=b, j=j: h.matmul(banks[b][:, 0:TT], lhsT=slot[:, kc, j * 128:(j + 1) * 128],
                                                          rhs=hT[:, kc, 0:TT], start=(kc == 0), stop=(kc == KC - 1)),
                       R=[r_hT, rs], W=[bres[b]], inc=(kc == KC - 1))
                handler(j, banks[b][:, 0:TT], bres[b])

        def silu_to(ps, br):
            i = nxt("tmpf", 2)
            ACT(lambda h: h.activation(out=tmpf[i][:, 0:TT], in_=ps, func=AF.Silu), R=[br, TU], W=[r_tmpf[i]])
            return tmpf[i][:, 0:TT], r_tmpf[i]

        def h_gate_a(half_i):
            def hh(j, ps, br):
                hd = half_i * 4 + j
                sg, rsg = silu_to(ps, br)
                DVE(lambda h: h.tensor_tensor(out=y_a[:, hd, 0:TT], in0=attnT[:, hd, 0:TT], in1=sg, op=ALU.mult),
                    R=[r_attn, rsg, TU], W=[r_ya])
            return hh

        def h_u(j, ps, br):
            ACT(lambda h: h.activation(out=ubuf[:, j, 0:TT], in_=ps, func=AF.Copy), R=[br, TU], W=[r_ubuf])

        def h_cgate(j, ps, br):
            if T.kind == "p":
                DVE(lambda h: h.tensor_copy(out=cin_v[:, j, 0, 0:2], in_=hist_c[:, j, :]), R=[r_histc, TU], W=[r_cin])
            else:
                for s in range(NSEG):
                    SP(cin_v[:, j, s, 0:2], sconv[l, s, :, j * 128:(j + 1) * 128].rearrange("t p -> p t"),
                       R=[TU], W=[r_cin], slow=True)
            DVE(lambda h: h.tensor_tensor(out=cin_v[:, j, :, 2:CL],
                                          in0=ps.rearrange("p (s t) -> p s t", s=NSEG),
                                          in1=ubuf[:, j, 0:TT].rearrange("p (s t) -> p s t", s=NSEG), op=ALU.mult),
                R=[br, r_ubuf, TU], W=[r_cin])
            if T.kind == "p":
                DVE(lambda h: h.tensor_copy(out=hist_c[:, j, :], in_=cin_v[:, j, 0, CL - 2:CL]), R=[r_cin, TU],
                    W=[r_histc])
                if T.last:
                    SP(convp[l, :, j * 128:(j + 1) * 128].rearrange("t p -> p t"), cin_v[:, j, 0, CL - 2:CL],
                       R=[r_cin, TU], slow=True)
            else:
                for s in range(NSEG):
                    SP(convs[l, s, :, j * 128:(j + 1) * 128].rearrange("t p -> p t"), cin_v[:, j, s, CL - 2:CL],
                       R=[r_cin, TU], slow=True)
            uv = ubuf[:, j, 0:TT].rearrange("p (s t) -> p s t", s=NSEG)
            DVE(lambda h: h.tensor_scalar(out=uv, in0=cin_v[:, j, :, 0:SEGL], scalar1=cwT[:, l, 0, j:j + 1],
                                           scalar2=cbT[:, l, j:j + 1], op0=ALU.mult, op1=ALU.add),
                 R=[r_cin, r_const, TU], W=[r_ubuf])
            DVE(lambda h: h.scalar_tensor_tensor(out=uv, in0=cin_v[:, j, :, 1:SEGL + 1],
                                                  scalar=cwT[:, l, 1, j:j + 1], in1=uv, op0=ALU.mult, op1=ALU.add),
                 R=[r_cin, r_const, r_ubuf, TU], W=[r_ubuf])
            DVE(lambda h: h.scalar_tensor_tensor(out=uv, in0=cin_v[:, j, :, 2:SEGL + 2],
                                                  scalar=cwT[:, l, 2, j:j + 1], in1=uv, op0=ALU.mult, op1=ALU.add),
                 R=[r_cin, r_const, r_ubuf, TU], W=[r_ubuf])

        def h_bgate(j, ps, br):
            DVE(lambda h: h.tensor_tensor(out=ubuf[:, j, 0:TT], in0=ps, in1=ubuf[:, j, 0:TT], op=ALU.mult),
                R=[br, r_ubuf, TU], W=[r_ubuf])

        def h_gate_b(j, ps, br):
            sg, rsg = silu_to(ps, br)
            DVE(lambda h: h.tensor_tensor(out=y_b[:, j, 0:TT], in0=ubuf[:, j, 0:TT], in1=sg, op=ALU.mult),
                R=[r_ubuf, rsg, TU], W=[r_yb])

        def h_pin(j, ps, br):
            if T.kind == "p":
                DVE(lambda h: h.tensor_copy(out=pbuf_v[:, j, 0, 0:POOL_HIST], in_=hist_p[:, j, :]),
                    R=[r_histp, TU], W=[r_pbuf])
            else:
                for s in range(NSEG):
                    SP(pbuf_v[:, j, s, 0:POOL_HIST], spool[l, s, :, j * 128:(j + 1) * 128].rearrange("t p -> p t"),
                       R=[TU], W=[r_pbuf], slow=True)
            ACT(lambda h: h.activation(out=pbuf_v[:, j, :, POOL_HIST:PL],
                                       in_=ps.rearrange("p (s t) -> p s t", s=NSEG), func=AF.Copy),
                R=[br, TU], W=[r_pbuf])
            if T.kind == "p":
                DVE(lambda h: h.tensor_copy(out=hist_p[:, j, :], in_=pbuf_v[:, j, 0, PL - POOL_HIST:PL]),
                    R=[r_pbuf, TU], W=[r_histp])
                if T.last:
                    SP(poolp[l, :, j * 128:(j + 1) * 128].rearrange("t p -> p t"),
                       pbuf_v[:, j, 0, PL - POOL_HIST:PL], R=[r_pbuf, TU], slow=True)
            else:
                for s in range(NSEG):
                    SP(pools[l, s, :, j * 128:(j + 1) * 128].rearrange("t p -> p t"),
                       pbuf_v[:, j, s, PL - POOL_HIST:PL], R=[r_pbuf, TU], slow=True)
            wav = wa[:, 0:NSEG * PL].rearrange("p (s t) -> p s t", s=NSEG)
            wbv = wb[:, 0:NSEG * PL].rearrange("p (s t) -> p s t", s=NSEG)
            src, rsrc = pbuf_v[:, j, :, :], r_pbuf
            bufs = [(wav, r_wa), (wbv, r_wb)]
            lo_i = 0
            for lev in range(j + 1):
                sh = 1 << lev
                dstv, rdst = bufs[lev % 2]
                a0 = lo_i + sh
                DVE(lambda h, src=src, dstv=dstv, a0=a0, sh=sh: h.tensor_tensor(
                    out=dstv[:, :, a0:PL], in0=src[:, :, a0:PL], in1=src[:, :, a0 - sh:PL - sh], op=ALU.add),
                    R=[rsrc, TU], W=[rdst])
                src, rsrc = dstv, rdst
                lo_i = a0
            win = float(1 << (j + 1))
            di = nxt("dbf", 2)
            dv = dbf[di][:, 0:TT].rearrange("p (s t) -> p s t", s=NSEG)
            DVE(lambda h, src=src: h.scalar_tensor_tensor(out=dv, in0=src[:, :, POOL_HIST:PL], scalar=1.0 / win,
                                                          in1=pbuf_v[:, j, :, POOL_HIST:PL], op0=ALU.mult,
                                                          op1=ALU.subtract), R=[rsrc, r_pbuf, TU], W=[r_dbf[di]])
            if T.kind == "p" and T.first:
                t16 = t16buf[:, :]
                DVE(lambda h, src=src: h.tensor_tensor(out=t16, in0=src[:, 0, POOL_HIST:POOL_HIST + 16],
                                                       in1=invc[:, j, :], op=ALU.mult), R=[rsrc, r_const, TU],
                    W=[r_rtmp])
                DVE(lambda h: h.tensor_tensor(out=dbf[di][:, 0:16], in0=t16,
                                              in1=pbuf_v[:, j, 0, POOL_HIST:POOL_HIST + 16], op=ALU.subtract),
                    R=[r_rtmp, r_pbuf, TU], W=[r_dbf[di]])
            b = 6 + (j % 2)
            PE(lambda h: h.matmul(banks[b][:, 0:TT], lhsT=poolw[:, l, j, :], rhs=dbf[di][:, 0:TT], start=True,
                                  stop=True), R=[r_const, r_dbf[di], TU], W=[bres[b]])
            ACT(lambda h: h.activation(out=mixs[:, j, 0:TT], in_=banks[b][:, 0:TT], func=AF.Copy,
                                       scale=psT[:, l, j:j + 1]), R=[bres[b], r_const, TU], W=[r_mixs])

        def h_gate_c(j, ps, br):
            sg, rsg = silu_to(ps, br)
            DVE(lambda h: h.tensor_tensor(out=y_c[:, j, 0:TT], in0=mixs[:, j, 0:TT], in1=sg, op=ALU.mult),
                R=[r_mixs, rsg, TU], W=[r_yc])

        fm_block(0, h_gate_a(0))
        fm_block(1, h_gate_a(1))
        fm_block(2, h_u)
        fm_block(4, h_cgate)
        fm_block(3, h_bgate)
        fm_block(5, h_gate_b)
        barrier()
        fm_block(6, h_pin)
        fm_block(7, h_gate_c)

        if l == 0:
            dump("dbg_ya_" + T.kind, y_a[:, :, 0:TT], [r_ya, TU])
            dump("dbg_yb_" + T.kind, y_b[:, :, 0:TT], [r_yb, TU])
            dump("dbg_yc_" + T.kind, y_c[:, :, 0:TT], [r_yc, TU])
        mark('L%d %s%d C5' % (l, T.kind, T.idx))
        barrier()
        SP(gfin[0:SUBn, :], xsrc[rows(0), :], R=[TU, r_x1], W=[r_gfin])
        ysrc = [(y_a, r_ya, 0, 8), (y_b, r_yb, 8, 4), (y_c, r_yc, 12, 4)]
        for nb in range(c.ND):
            nxt("wslot", 4)
            lslot, lrs = load_w(l, blk_lift(nb))
            for bx in range(3):
                mslot, mrs = load_w(l, blk_fm(8 + bx * c.ND + nb))
                yv, ry, k0, nk = ysrc[bx]
                for j in range(4):
                    b = mmbank()
                    for kc in range(KC):
                        PE(lambda h, kc=kc, b=b, j=j: h.matmul(banks[b][:, 0:TT],
                                                              lhsT=mslot[:, kc, j * 128:(j + 1) * 128],
                                                              rhs=hT[:, kc, 0:TT], start=(kc == 0),
                                                              stop=(kc == KC - 1)),
                           R=[r_hT, mrs], W=[bres[b]], inc=(kc == KC - 1))
                    si = nxt("sig", 2)
                    ACT(lambda h, b=b, si=si: h.activation(out=sigb[si][:, 0:TT], in_=banks[b][:, 0:TT],
                                                           func=AF.Sigmoid), R=[bres[b], TU], W=[r_sig[si]])
                    b2 = trbank()
                    for kk in range(nk):
                        PE(lambda h, kk=kk, b2=b2, j=j: h.matmul(banks[b2][:, 0:TT],
                                                                lhsT=lslot[:, k0 + kk, j * 128:(j + 1) * 128],
                                                                rhs=yv[:, kk, 0:TT], start=(kk == 0),
                                                                stop=(kk == nk - 1)),
                           R=[ry, lrs, TU], W=[bres[b2]], inc=(kk == nk - 1))
                    if bx == 0:
                        DVE(lambda h, b2=b2, si=si, j=j: h.tensor_tensor(out=zacc[:, j, 0:TT], in0=banks[b2][:, 0:TT],
                                                                        in1=sigb[si][:, 0:TT], op=ALU.mult),
                            R=[bres[b2], r_sig[si], TU], W=[r_zacc])
                    else:
                        zi = nxt("zt", 2)
                        DVE(lambda h, b2=b2, si=si, zi=zi: h.tensor_tensor(out=ztmp[zi][:, 0:TT],
                                                                          in0=banks[b2][:, 0:TT],
                                                                          in1=sigb[si][:, 0:TT], op=ALU.mult),
                            R=[bres[b2], r_sig[si], TU], W=[r_ztmp[zi]])
                        if bx == 1:
                            DVE(lambda h, zi=zi, j=j: h.tensor_tensor(out=zacc[:, j, 0:TT], in0=zacc[:, j, 0:TT],
                                                                      in1=ztmp[zi][:, 0:TT], op=ALU.add),
                                 R=[r_ztmp[zi], r_zacc, TU], W=[r_zacc])
                        else:
                            DVE(lambda h, zi=zi, j=j: h.tensor_tensor(out=zT[:, nb * 4 + j, 0:TT],
                                                                      in0=zacc[:, j, 0:TT], in1=ztmp[zi][:, 0:TT],
                                                                      op=ALU.add),
                                 R=[r_ztmp[zi], r_zacc, TU], W=[r_zT])

        if l == 0:
            dump("dbg_z_" + T.kind, zT[:, :, 0:TT], [r_zT, TU])
        mark('L%d %s%d C6' % (l, T.kind, T.idx))
        barrier()
        xr = gfin
        for nb in range(c.ND):
            slot, rs = load_w(l, blk_out(nb))
            for st in range(NT):
                b = nxt("c6", 8)
                for kc in range(KC):
                    PE(lambda h, kc=kc, b=b, st=st: h.matmul(banks[b][0:SUBn, :], lhsT=zT[:, kc, tl(st)],
                                                          rhs=slot[:, kc, :], start=(kc == 0), stop=(kc == KC - 1)),
                       R=[r_zT, rs, TU], W=[bres[b]], inc=(kc == KC - 1))
                ACT(lambda h, b=b, st=st, nb=nb: h.activation(out=xo[0:SUBn, st, nb * 512:(nb + 1) * 512],
                                                             in_=banks[b][0:SUBn, :], func=AF.Copy),
                    R=[bres[b], TU], W=[r_xo[st]])
        hstage = [hT[:, 0:8, :].rearrange("p a b -> p (a b)").bitcast(F32),
                  hT[:, 8:16, :].rearrange("p a b -> p (a b)").bitcast(F32)]
        stg = {}
        for st in range(NT):
            if st == 0:
                stg[st] = (xr, r_gfin)
            elif st in (1, 2) and KC == 16:
                stg[st] = (hstage[st - 1], r_hT)
                SP(hstage[st - 1][0:SUBn, :], xsrc[rows(st), :], R=[r_x1], W=[r_hT])
            else:
                stg[st] = (xr, r_gfin)
        for st in range(NT):
            buf, rb = stg[st]
            if st > 0 and buf is xr:
                SP(xr[0:SUBn, :], xsrc[rows(st), :], R=[TU, r_x1], W=[r_gfin])
            DVE(lambda h, st=st, buf=buf: h.tensor_tensor(out=xo[0:SUBn, st, :], in0=xo[0:SUBn, st, :],
                                                         in1=buf[0:SUBn, :], op=ALU.add),
                R=[rb, r_xo[st], TU], W=[r_xo[st]])
            if not last_layer:
                SP(xdst[rows(st), :], xo[0:SUBn, st, :], R=[r_xo[st], TU], W=[r_x1])
        if last_layer:
            SP(gfin[:, :], fng.partition_broadcast(128), R=[TU], W=[r_gfin])
            for st in range(NT):
                DVE(lambda h: h.memset(small[0:SUBn, 10:11], 0.0), W=[r_small])
                ACT(lambda h, st=st: h.activation(out=hT[0:SUBn, 0:4, :].rearrange("p a b -> p (a b)")[:, 0:D],
                                                 in_=xo[0:SUBn, st, :], func=AF.Square,
                                                 accum_out=small[0:SUBn, 10:11]), R=[r_xo[st], r_small, TU],
                    W=[r_hT, r_small])
                ACT(lambda h: h.activation(out=small[0:SUBn, 11:12], in_=small[0:SUBn, 10:11], func=AF.Sqrt,
                                           bias=EPS_AP[0:SUBn, :], scale=1.0 / D), R=[r_small, r_const], W=[r_small])
                DVE(lambda h: h.reciprocal(out=small[0:SUBn, 11:12], in_=small[0:SUBn, 11:12]), R=[r_small],
                    W=[r_small])
                DVE(lambda h, st=st: h.scalar_tensor_tensor(out=xo[0:SUBn, st, :], in0=xo[0:SUBn, st, :],
                                                           scalar=small[0:SUBn, 11:12], in1=gfin[0:SUBn, :],
                                                           op0=ALU.mult, op1=ALU.mult),
                    R=[r_xo[st], r_small, r_gfin, TU], W=[r_xo[st]])
                SP(xdst[rows(st), :], xo[0:SUBn, st, :], R=[r_xo[st], TU])

    def load_sample_cache(l, b):
        npast_t = PAST // 128
        GDMA(V[:, 0:npast_t, :], cv[l, b].rearrange("(t p) c -> p t c", p=128), W=[r_V])
        for r0 in range(0, npast_t, 8):
            nr = min(8, npast_t - r0)
            GDMA(kstg[:, 0:nr, :], ck[l, b, r0 * 128:(r0 + nr) * 128, :].rearrange("(t p) c -> p t c", p=128),
                 R=[TU], W=[r_kstg])
            for dup in range(2):
                GDMA(ikstg[:, 0:nr, dup, :],
                     cik[l, b, r0 * 128:(r0 + nr) * 128, :].rearrange("(t p) c -> p t c", p=128), R=[TU], W=[r_ikstg])
            for g in range(2):
                def ev(pv, br2, g=g, r0=r0, nr=nr):
                    ACT(lambda h: h.activation(out=KT[:, g, r0 * 128:(r0 + nr) * 128].rearrange(
                        "p (j t) -> p j t", j=nr), in_=pv, func=AF.Copy), R=[br2], W=[r_KT])
                transposes(lambda jj, g=g: kstg[:, jj, g * 128:(g + 1) * 128], nr, 128, 128, ev, [r_kstg])

            def ev3(pv, br2, r0=r0, nr=nr):
                ACT(lambda h: h.activation(out=IKT[:, r0 * 128:(r0 + nr) * 128].rearrange("p (j t) -> p j t", j=nr),
                                           in_=pv, func=AF.Copy), R=[br2], W=[r_IKT])
            transposes(lambda jj: ikstg[:, jj, :, :].rearrange("p a d -> p (a d)"), nr, 128, 128, ev3, [r_ikstg])
        DVE(lambda h: h.tensor_copy(out=KT[:, :, PAST:PAST + DEC], in_=knew[:, b, :, :]), R=[r_new], W=[r_KT])
        DVE(lambda h: h.tensor_copy(out=IKT[:, PAST:PAST + DEC], in_=iknew[:, b, :]), R=[r_new], W=[r_IKT])
        DVE(lambda h: h.tensor_copy(out=V[0:DEC, npast_t, :], in_=vnew[0:DEC, b, :]), R=[r_new], W=[r_V])

    EPS_AP = sb("eps_ap", [128, 1], F32)
    DVE(lambda h: h.memset(EPS_AP[:, :], EPS), W=[r_const])

    for l in range(DEPTH):
        for T in tiles:
            process(l, T)

    mark('END')
    S.finish(sp)
    S.finish(pool)

    sem_handles = [es.enter_context(nc.semaphore(n)) for n in S.sems]
    block = es.enter_context(nc.Block())

    def replay(e, h):
        for item in e.prog:
            if item[0] == "wait":
                h.wait_ge(sem_handles[item[1]], item[2])
            else:
                _, call, sk, incv = item
                ins = getattr(h, call[0])(*call[1], **call[2])
                if sk is not None:
                    ins.then_inc(sem_handles[sk], incv)

    @block.tensor
    def _(h):
        replay(pe, h)

    @block.scalar
    def _(h):
        replay(act, h)

    @block.vector
    def _(h):
        replay(dve, h)

    @block.gpsimd
    def _(h):
        replay(pool, h)

    @block.sync
    def _(h):
        replay(sp, h)

    es.close()
    return nc


def _rope_tables(pos, half):
    inv = (np.float32(THETA) ** (-np.arange(half, dtype=np.float32) / np.float32(half))).astype(np.float32)
    ang = pos.astype(np.float32)[:, None] * inv[None, :]
    return np.concatenate([np.cos(ang), np.sin(ang)], axis=1).astype(np.float32)


def make_consts(cfg):
    c = cfg
    pos_p = np.arange(c.SEQ)
    pos_s = c.PAST + np.arange(c.DEC)
    pos_s2 = np.concatenate([pos_s, pos_s])
    invc = np.zeros((128, 4, 16), np.float32)
    for g in range(4):
        w = 2 << g
        invc[:, g, :] = 1.0 / np.minimum(w, np.arange(16) + 1).astype(np.float32)
    pow2 = np.tile((0.5 ** (np.arange(NIT) + 1)).astype(np.float32)[None, :], (128, 1))
    return {
        "ident": np.eye(128, dtype=np.float32),
        "ropeq_p": _rope_tables(pos_p, 64), "ropei_p": _rope_tables(pos_p, 32),
        "ropeq_s": _rope_tables(pos_s2, 64), "ropei_s": _rope_tables(pos_s2, 32),
        "invc": invc, "pow2": pow2,
    }


def make_in_maps(cfg, n_cores, inp):
    c = cfg
    consts = make_consts(c)
    f = lambda a: np.ascontiguousarray(np.asarray(a, dtype=np.float32))
    shared = {k: f(inp[k]) for k in ("norm_g", "w_in", "conv_w", "conv_b", "pool_w", "pool_scale", "lift_a",
                                     "lift_b", "lift_c", "w_out")}
    shared["fng"] = f(inp["final_norm_g"])
    shared.update(consts)
    maps = []
    for i in range(n_cores):
        m = dict(shared)
        m["xp"] = f(inp["x_prompt"][i])
        sl = slice(2 * i, 2 * i + 2)
        m["xs"] = f(np.asarray(inp["x_sample"])[sl].reshape(2 * c.DEC, c.D))
        m["ck"] = f(np.asarray(inp["cache_k"])[:, sl].reshape(c.DEPTH, 2, c.PAST, KV_W))
        m["cv"] = f(np.asarray(inp["cache_v"])[:, sl].reshape(c.DEPTH, 2, c.PAST, KV_W))
        m["cik"] = f(np.asarray(inp["cache_idx_k"])[:, sl])
        m["sconv"] = f(np.asarray(inp["state_conv"])[:, sl])
        m["spool"] = f(np.asarray(inp["state_pool"])[:, sl])
        maps.append(m)
    return maps


def gather(cfg, n_cores, results):
    c = cfg
    R = results
    cat = lambda k, ax: np.stack([np.asarray(r[k]) for r in R], axis=ax)
    y_prompt = cat("yp", 0)
    y_sample = np.concatenate([np.asarray(r["ys"]).reshape(2, c.DEC, c.D) for r in R], axis=0)
    k_prompt = cat("kp", 1).reshape(c.DEPTH, n_cores, c.SEQ, 2, HEAD_DIM)
    v_prompt = cat("vp", 1).reshape(c.DEPTH, n_cores, c.SEQ, 2, HEAD_DIM)
    idxk_prompt = cat("ikp", 1)
    conv_prompt = cat("convp", 1)
    pool_prompt = cat("poolp", 1)
    catb = lambda k, shp: np.concatenate([np.asarray(r[k]).reshape((c.DEPTH, 2) + shp) for r in R], axis=1)
    k_sample = catb("ks", (c.DEC, 2, HEAD_DIM))
    v_sample = catb("vs", (c.DEC, 2, HEAD_DIM))
    idxk_sample = catb("iks", (c.DEC, IDX_DIM))
    conv_sample = catb("convs", (2, CONV_W))
    pool_sample = catb("pools", (POOL_HIST, POOL_W))
    outs = (y_prompt, y_sample, k_prompt, v_prompt, idxk_prompt, conv_prompt, pool_prompt,
            k_sample, v_sample, idxk_sample, conv_sample, pool_sample)
    return tuple(np.ascontiguousarray(o.astype(np.float32)) for o in outs)


def run(cfg, n_cores, inp):
    nc = build(cfg)
    maps = make_in_maps(cfg, n_cores, inp)
    res = run_bass_kernel_spmd(nc, maps, core_ids=list(range(n_cores)))
    return gather(cfg, n_cores, res.results)


def kernel(**inputs):
    return run(Cfg(), N_CORES, inputs)
```
